# Optimizing a Trainium2 kernel written in Bass

```python
import math
import jax
import jax.numpy as jnp
from jax import lax
import numpy as np

D_MODEL = 2048
BATCH = 8
SEQ = 2048
DEPTH = 1
DEC_BATCH = 2
DEC_SEQ = 8192
PAST_LEN = 128

N_HEADS = 4
DH = 128
ATTN_QK_WIDTH = N_HEADS * 2 * DH
ATTN_V_WIDTH = N_HEADS * 2 * DH
ROT_DIM = DH // 4
ROPE_THETA = 500000.0
Q_BLOCK = 128
ATTN_SCALE = DH ** -0.5
CHUNK = 128
SGU_GROUPS = 8
SGU_GROUP_DIM = 128
SGU_WIDTH = SGU_GROUPS * SGU_GROUP_DIM
D_FF = 4 * D_MODEL
SPLIT_SIZES = (ATTN_QK_WIDTH, ATTN_QK_WIDTH, ATTN_V_WIDTH, SGU_WIDTH, SGU_WIDTH, D_MODEL, D_MODEL)
SPLIT_POINTS = tuple(int(s) for s in np.cumsum(SPLIT_SIZES)[:-1])
IN_COLS = sum(SPLIT_SIZES)
RMS_EPS = 1e-6
LN_EPS = 1e-5

kernel_name = "hybrid_diffattn_sgu_encoder"


def lambda_init_fn(layer_idx):
    return 0.8 - 0.6 * math.exp(-0.3 * layer_idx)


def rmsnorm(x, g, eps=RMS_EPS):
    xf = x.astype(jnp.float32)
    y = xf * lax.rsqrt(jnp.mean(xf * xf, axis=-1, keepdims=True) + eps)
    return (y * g.astype(jnp.float32)).astype(x.dtype)


def layernorm(x, g, b, eps=LN_EPS):
    xf = x.astype(jnp.float32)
    mu = jnp.mean(xf, axis=-1, keepdims=True)
    xc = xf - mu
    y = xc * lax.rsqrt(jnp.mean(xc * xc, axis=-1, keepdims=True) + eps)
    return (y * g.astype(jnp.float32) + b.astype(jnp.float32)).astype(x.dtype)


def partial_rotary(t, seq_len):
    pos = jnp.arange(seq_len, dtype=jnp.float32)
    inv_freq = 1.0 / (jnp.float32(ROPE_THETA) ** (jnp.arange(0, ROT_DIM, 2, dtype=jnp.float32) / ROT_DIM))
    ang = pos[:, None] * inv_freq[None, :]
    cos = jnp.cos(ang)[None, :, None, None, :]
    sin = jnp.sin(ang)[None, :, None, None, :]
    tf = t.astype(jnp.float32)
    half = ROT_DIM // 2
    x1 = tf[..., :half]
    x2 = tf[..., half:ROT_DIM]
    rot = jnp.concatenate([x1 * cos - x2 * sin, x2 * cos + x1 * sin, tf[..., ROT_DIM:]], axis=-1)
    return rot.astype(t.dtype)


def diff_attention(q, k, v, lam):
    B, S = q.shape[0], q.shape[1]
    nb = S // Q_BLOCK
    qb = q.reshape(B, nb, Q_BLOCK, N_HEADS, 2, DH).transpose(1, 0, 2, 3, 4, 5)
    lam32 = lam.astype(jnp.float32)

    def one_block(qblk):
        s = jnp.einsum('bqhcd,bkhcd->bhcqk', qblk, k,
                       preferred_element_type=jnp.float32) * ATTN_SCALE
        p = jax.nn.softmax(s, axis=-1)
        w = (p[:, :, 0] - lam32 * p[:, :, 1]).astype(v.dtype)
        return jnp.einsum('bhqk,bkhe->bqhe', w, v)

    o = lax.map(one_block, qb)
    return o.transpose(1, 0, 2, 3, 4).reshape(B, S, N_HEADS, 2 * DH)


def spatial_gating(su, sv, ln_g, ln_b, sgu_w, sgu_b):
    B, S = su.shape[0], su.shape[1]
    sv = layernorm(sv, ln_g, ln_b)
    svc = sv.reshape(B, S // CHUNK, CHUNK, SGU_GROUPS, SGU_GROUP_DIM)
    mixed = jnp.einsum('gpq,bnqgc->bnpgc', sgu_w, svc) + sgu_b.T[:, :, None]
    return su * mixed.reshape(B, S, SGU_WIDTH)


def encoder_layer(x, layer_idx, attn_norm_g, w_in, lambda_q1, lambda_k1, lambda_q2, lambda_k2,
                  subln_g, sgu_ln_g, sgu_ln_b, sgu_w, sgu_b, w_br_attn, w_br_sgu, w_out,
                  mlp_norm_g, w_up, w_down):
    B, S = x.shape[0], x.shape[1]
    lambda_init = lambda_init_fn(layer_idx)
    h = rmsnorm(x, attn_norm_g)
    z = h @ w_in
    q, k, v, su, sv, g_attn, g_sgu = jnp.split(z, SPLIT_POINTS, axis=-1)

    q = partial_rotary(q.reshape(B, S, N_HEADS, 2, DH), S)
    k = partial_rotary(k.reshape(B, S, N_HEADS, 2, DH), S)
    v = v.reshape(B, S, N_HEADS, 2 * DH)
    lam = (jnp.exp(jnp.sum(lambda_q1.astype(jnp.float32) * lambda_k1.astype(jnp.float32)))
           - jnp.exp(jnp.sum(lambda_q2.astype(jnp.float32) * lambda_k2.astype(jnp.float32)))
           + lambda_init)
    o = diff_attention(q, k, v, lam)
    o_attn = (rmsnorm(o, subln_g) * (1.0 - lambda_init)).reshape(B, S, ATTN_V_WIDTH)

    o_sgu = spatial_gating(jax.nn.gelu(su, approximate=False), jax.nn.gelu(sv, approximate=False),
                           sgu_ln_g, sgu_ln_b, sgu_w, sgu_b)

    merged = jax.nn.sigmoid(g_attn) * (o_attn @ w_br_attn) + jax.nn.sigmoid(g_sgu) * (o_sgu @ w_br_sgu)
    x = x + merged @ w_out

    h2 = rmsnorm(x, mlp_norm_g)
    x = x + jnp.square(jax.nn.relu(h2 @ w_up)) @ w_down
    return x


def encoder_trunk(x, attn_norm_g, w_in, lambda_q1, lambda_k1, lambda_q2, lambda_k2, subln_g,
                  sgu_ln_g, sgu_ln_b, sgu_w, sgu_b, w_br_attn, w_br_sgu, w_out, mlp_norm_g,
                  w_up, w_down, final_norm_g):
    for l in range(DEPTH):
        x = encoder_layer(x, l, attn_norm_g[l], w_in[l], lambda_q1[l], lambda_k1[l], lambda_q2[l],
                          lambda_k2[l], subln_g[l], sgu_ln_g[l], sgu_ln_b[l], sgu_w[l], sgu_b[l],
                          w_br_attn[l], w_br_sgu[l], w_out[l], mlp_norm_g[l], w_up[l], w_down[l])
    return rmsnorm(x, final_norm_g)


def setup_inputs(seed: int = 0) -> dict:
    key = jax.random.key(seed)
    ks = jax.random.split(key, 22)
    f32 = jnp.float32
    n = lambda k, shape, s: jax.random.normal(k, shape, f32) * s
    return {
        "x_prompt": jax.random.normal(ks[0], (BATCH, SEQ, D_MODEL), f32),
        "x_sample": jax.random.normal(ks[1], (DEC_BATCH, DEC_SEQ, D_MODEL), f32),
        "attn_norm_g": 1.0 + n(ks[2], (DEPTH, D_MODEL), 0.02),
        "w_in": n(ks[3], (DEPTH, D_MODEL, IN_COLS), D_MODEL ** -0.5),
        "lambda_q1": n(ks[4], (DEPTH, DH), 0.1),
        "lambda_k1": n(ks[5], (DEPTH, DH), 0.1),
        "lambda_q2": n(ks[6], (DEPTH, DH), 0.1),
        "lambda_k2": n(ks[7], (DEPTH, DH), 0.1),
        "subln_g": 1.0 + n(ks[8], (DEPTH, 2 * DH), 0.02),
        "sgu_ln_g": 1.0 + n(ks[9], (DEPTH, SGU_WIDTH), 0.02),
        "sgu_ln_b": n(ks[10], (DEPTH, SGU_WIDTH), 0.02),
        "sgu_w": n(ks[11], (DEPTH, SGU_GROUPS, CHUNK, CHUNK), CHUNK ** -0.5),
        "sgu_b": n(ks[12], (DEPTH, SGU_GROUPS, CHUNK), 0.02),
        "w_br_attn": n(ks[13], (DEPTH, ATTN_V_WIDTH, D_MODEL), ATTN_V_WIDTH ** -0.5),
        "w_br_sgu": n(ks[14], (DEPTH, SGU_WIDTH, D_MODEL), SGU_WIDTH ** -0.5),
        "w_out": n(ks[15], (DEPTH, D_MODEL, D_MODEL), D_MODEL ** -0.5),
        "mlp_norm_g": 1.0 + n(ks[16], (DEPTH, D_MODEL), 0.02),
        "w_up": n(ks[17], (DEPTH, D_MODEL, D_FF), D_MODEL ** -0.5),
        "w_down": n(ks[18], (DEPTH, D_FF, D_MODEL), D_FF ** -0.5),
        "final_norm_g": 1.0 + n(ks[19], (D_MODEL,), 0.02),
    }


def reference(x_prompt, x_sample, attn_norm_g, w_in, lambda_q1, lambda_k1, lambda_q2, lambda_k2,
              subln_g, sgu_ln_g, sgu_ln_b, sgu_w, sgu_b, w_br_attn, w_br_sgu, w_out, mlp_norm_g,
              w_up, w_down, final_norm_g):
    y_prompt = encoder_trunk(x_prompt, attn_norm_g, w_in, lambda_q1, lambda_k1, lambda_q2, lambda_k2,
                             subln_g, sgu_ln_g, sgu_ln_b, sgu_w, sgu_b, w_br_attn, w_br_sgu, w_out,
                             mlp_norm_g, w_up, w_down, final_norm_g)
    y_sample = encoder_trunk(x_sample, attn_norm_g, w_in, lambda_q1, lambda_k1, lambda_q2, lambda_k2,
                             subln_g, sgu_ln_g, sgu_ln_b, sgu_w, sgu_b, w_br_attn, w_br_sgu, w_out,
                             mlp_norm_g, w_up, w_down, final_norm_g)
    return (y_prompt, y_sample)
```

```python
import contextlib
import math
import numpy as np
import ml_dtypes
import concourse.bass as bass
import concourse.mybir as mybir
from concourse.bass_utils import run_bass_kernel_spmd

F32 = mybir.dt.float32
BF16 = mybir.dt.bfloat16
AF = mybir.ActivationFunctionType
ALU = mybir.AluOpType

ENGS = ("pe", "act", "dve", "pool", "sp")

D = 2048
KD = 16
DFF = 8192
T = 512
ATTN_SCALE = 128 ** -0.5
RMS_EPS = 1e-6
LN_EPS = 1e-5
LAMBDA_INIT = 0.8 - 0.6 * math.exp(-0.3 * 0)
ROPE_THETA = 500000.0


class Res:
    __slots__ = ("w", "rs", "name", "excl")

    def __init__(self, name="", excl=False):
        self.w = None
        self.rs = []
        self.name = name
        self.excl = excl


class Op:
    __slots__ = ("eng", "fn", "deps", "isdma", "key", "ninc", "sig", "needed", "waits", "idx")


class Sched:
    def __init__(self, nc, stack, n_dma_sems=60):
        self.nc = nc
        self.ops = {e: [] for e in ENGS}
        self.esem = {e: stack.enter_context(nc.semaphore("s_" + e)) for e in ENGS if e != "sp"}
        self.dma_pool = [stack.enter_context(nc.semaphore("d%d" % i)) for i in range(n_dma_sems)]
        self.dma_keys = {}
        self.nops = 0

    def _mk(self, eng, fn, reads, writes, isdma=False, key=None, ninc=1):
        op = Op()
        op.eng = eng
        op.fn = fn
        op.isdma = isdma
        op.key = key
        op.ninc = ninc
        op.sig = None
        op.needed = False
        op.waits = None
        op.idx = self.nops
        self.nops += 1
        deps = set()
        pe_plain = (eng == "pe" and not isdma)
        if any(r.excl for r in reads):
            writes = list(writes) + [r for r in reads if r.excl]
            reads = [r for r in reads if not r.excl]
        for r in reads:
            d = r.w
            if d is not None and not (pe_plain and d.eng == "pe" and not d.isdma):
                deps.add(d)
        for w in writes:
            d = w.w
            if d is not None and (d.isdma or isdma or d.eng != eng):
                deps.add(d)
            for d in w.rs:
                if d.isdma or isdma or d.eng != eng:
                    deps.add(d)
        op.deps = deps
        for d in deps:
            d.needed = True
        for r in reads:
            rs = r.rs
            if rs and rs[-1].eng == eng and not isdma and not rs[-1].isdma:
                rs[-1] = op
            else:
                rs.append(op)
        for w in writes:
            w.w = op
            w.rs = []
        self.ops[eng].append(op)
        return op

    def op(self, eng, fn, reads=(), writes=()):
        return self._mk(eng, fn, reads, writes)

    def dma(self, queue, pairs, reads=(), writes=(), key=None, **kw):
        def fn(eng, sem, pairs=pairs, kw=kw):
            for (o, i) in pairs:
                eng.dma_start(out=o, in_=i, **kw).then_inc(sem, 16)
        return self._mk(queue, fn, reads, writes, isdma=True, key=key, ninc=len(pairs))

    def finalize(self, final_wait_ops=()):
        if final_wait_ops:
            fop = Op()
            fop.eng = "sp"; fop.fn = None; fop.isdma = False; fop.key = None; fop.ninc = 0
            fop.sig = None; fop.needed = False; fop.waits = None; fop.idx = self.nops
            fop.deps = set(final_wait_ops)
            for d in fop.deps:
                d.needed = True
            self.ops["sp"].append(fop)
        cnt = {e: 0 for e in ENGS}
        keyq = {}
        for e in ENGS:
            for op in self.ops[e]:
                if op.isdma:
                    if op.key is None:
                        op.key = ("auto", op.idx)
                    assert keyq.setdefault(op.key, e) == e, "dma key used on two queues"
                    if op.key not in self.dma_keys:
                        if not self.dma_pool:
                            raise RuntimeError("out of dma semaphores")
                        self.dma_keys[op.key] = [self.dma_pool.pop(), 0]
                    ent = self.dma_keys[op.key]
                    ent[1] += 16 * op.ninc
                    op.sig = (ent[0], ent[1])
                elif op.needed:
                    cnt[e] += 1
                    op.sig = (self.esem[e], cnt[e])
        for e in ENGS:
            waited = {}
            for op in self.ops[e]:
                need = {}
                for d in op.deps:
                    sem, val = d.sig
                    k = id(sem)
                    if k not in need or need[k][1] < val:
                        need[k] = (sem, val)
                ws = []
                for k, (sem, val) in need.items():
                    if waited.get(k, 0) >= val:
                        continue
                    waited[k] = val
                    ws.append((sem, val))
                op.waits = ws
        self.counts = cnt

    def emit(self):
        nc = self.nc
        with nc.Block() as block:
            def mk(e):
                def body(eng):
                    for op in self.ops[e]:
                        for (sem, val) in op.waits:
                            eng.wait_ge(sem, val)
                        if op.fn is None:
                            continue
                        if op.isdma:
                            op.fn(eng, op.sig[0])
                        else:
                            ins = op.fn(eng)
                            if op.needed:
                                ins.then_inc(op.sig[0], 1)
                return body
            block.tensor(mk("pe"))
            block.scalar(mk("act"))
            block.vector(mk("dve"))
            block.gpsimd(mk("pool"))
            block.sync(mk("sp"))


class Arena:
    GR = 1024

    def __init__(self, nc, stack, nbytes, name="arena"):
        self.nbytes = nbytes
        self.t = stack.enter_context(nc.sbuf_tensor(name, [128, nbytes // 2], BF16))
        self.res = [Res("%s_%d" % (name, i)) for i in range((nbytes + self.GR - 1) // self.GR)]

    def view(self, off, nbytes, dtype=BF16, parts=128):
        assert off % 4 == 0 and nbytes % 2 == 0 and off + nbytes <= self.nbytes, (off, nbytes)
        ap = self.t[0:parts, off // 2:(off + nbytes) // 2]
        if dtype != BF16:
            ap = ap.bitcast(dtype)
        return ap, self.res[off // self.GR:(off + nbytes - 1) // self.GR + 1]


class Buf:
    def __init__(self, arena, off, nelem, dtype):
        self.a = arena
        self.off = off
        self.dtype = dtype
        self.es = 4 if dtype == F32 else 2
        self.n = nelem

    def sub(self, e0, n, parts=128):
        assert e0 + n <= self.n
        return self.a.view(self.off + e0 * self.es, n * self.es, self.dtype, parts)

    def all(self, parts=128):
        return self.a.view(self.off, self.n * self.es, self.dtype, parts)


PANELS = ([("in", g, 16) for g in range(18)] + [("bra", g, 8) for g in range(4)] +
          [("brs", g, 8) for g in range(4)] + [("out", g, 16) for g in range(4)] +
          [("up", g, 16) for g in range(16)] + [("down", g * 4 + s, 16) for g in range(4) for s in range(4)])


class Builder:
    def __init__(self, NP, SS):
        assert NP % T == 0 and SS % (4 * T) == 0
        self.NP, self.SS, self.NS = NP, SS, SS // 4
        nc = self.nc = bass.Bass("TRN2", target_bir_lowering=False)
        dt = nc.dram_tensor
        NS = self.NS
        I = {}
        I["xp"] = dt("xp", [NP, D], F32, kind="ExternalInput").ap()
        I["xsf"] = dt("xsf", [SS, D], F32, kind="ExternalInput").ap()
        I["xso"] = dt("xso", [NS, D], F32, kind="ExternalInput").ap()
        I["w_in"] = dt("w_in", [D, 9216], F32, kind="ExternalInput").ap()
        I["w_bra"] = dt("w_bra", [1024, D], F32, kind="ExternalInput").ap()
        I["w_brs"] = dt("w_brs", [1024, D], F32, kind="ExternalInput").ap()
        I["w_out"] = dt("w_out", [D, D], F32, kind="ExternalInput").ap()
        I["w_up"] = dt("w_up", [D, DFF], F32, kind="ExternalInput").ap()
        I["w_down"] = dt("w_down", [DFF, D], F32, kind="ExternalInput").ap()
        I["g1"] = dt("g1", [128, 16], F32, kind="ExternalInput").ap()
        I["g2"] = dt("g2", [128, 16], F32, kind="ExternalInput").ap()
        I["gf"] = dt("gf", [128, D], F32, kind="ExternalInput").ap()
        I["lamv"] = dt("lamv", [128, 512], F32, kind="ExternalInput").ap()
        I["subg"] = dt("subg", [128, 2], F32, kind="ExternalInput").ap()
        I["lng"] = dt("lng", [128, 1024], F32, kind="ExternalInput").ap()
        I["lnb"] = dt("lnb", [128, 1024], F32, kind="ExternalInput").ap()
        I["sguw"] = dt("sguw", [128, 1024], F32, kind="ExternalInput").ap()
        I["sgub"] = dt("sgub", [128, 1024], F32, kind="ExternalInput").ap()
        I["ident"] = dt("ident", [128, 128], BF16, kind="ExternalInput").ap()
        I["pswap"] = dt("pswap", [128, 128], BF16, kind="ExternalInput").ap()
        I["ropeP"] = dt("ropeP", [32, 2, NP], F32, kind="ExternalInput").ap()
        I["ropeSF"] = dt("ropeSF", [32, 2, SS], F32, kind="ExternalInput").ap()
        I["ropeSO"] = dt("ropeSO", [32, 2, NS], F32, kind="ExternalInput").ap()
        self.I = I
        self.yp = dt("yp", [NP, D], F32, kind="ExternalOutput").ap()
        self.ys = dt("ys", [NS, D], F32, kind="ExternalOutput").ap()
        self.wscr = {}
        self.wres = {}
        for (nm, g, kc) in PANELS:
            self.wscr[(nm, g)] = dt("ws_%s_%d" % (nm, g), [128, kc * 512], BF16, kind="Internal").ap()
            self.wres[(nm, g)] = Res("ws_%s_%d" % (nm, g))
        self.kscr = {"p": dt("kscr_p", [4, 128, 2, NP], BF16, kind="Internal").ap(),
                     "s": dt("kscr_s", [4, 128, 2, SS], BF16, kind="Internal").ap()}
        self.vscr = {"p": dt("vscr_p", [4, 128, NP // 128, 258], BF16, kind="Internal").ap(),
                     "s": dt("vscr_s", [4, 128, SS // 128, 258], BF16, kind="Internal").ap()}
        self.kvres = {"p": Res("kv_p"), "s": Res("kv_s")}

    def build(self):
        nc = self.nc
        with contextlib.ExitStack() as st:
            self.S = S = Sched(nc, st)
            ARENA = 204 * 1024
            self.A = A = Arena(nc, st, ARENA)
            self.banks = [st.enter_context(nc.psum_tensor("bank%d" % i, [128, 512], F32)) for i in range(8)]
            self.bres = [Res("bank%d" % i, excl=True) for i in range(8)]
            self.bankptr = 0
            off = 0

            def alloc(nbytes):
                nonlocal off
                o = off
                off += (nbytes + 63) // 64 * 64
                return o
            K1 = 1024
            self.c_ident = Buf(A, alloc(256), 128, BF16)
            self.c_pswap = Buf(A, alloc(256), 128, BF16)
            self.c_g1 = Buf(A, alloc(64), 16, F32)
            self.c_g2 = Buf(A, alloc(64), 16, F32)
            self.c_subg = Buf(A, alloc(64), 2, F32)
            off = K1
            self.c_stat = Buf(A, alloc(1024), 256, F32)
            self.c_sguw = Buf(A, alloc(2048), 1024, BF16)
            self.c_sgub = Buf(A, alloc(4096), 1024, F32)
            self.c_lamv = Buf(A, alloc(2048), 512, F32)
            self.rope = Buf(A, alloc(4096), 1024, F32)
            self.E = [Buf(A, alloc(1024), 512, BF16) for _ in range(4)]
            self.o0 = [Buf(A, alloc(4096), 1024, F32) for _ in range(2)]
            self.on = Buf(A, alloc(2048), 1024, BF16)
            self.rt = [Buf(A, alloc(2048), 512, F32) for _ in range(2)]
            self.relu = [Buf(A, alloc(1024), 512, BF16) for _ in range(2)]
            self.ring = [Buf(A, alloc(16384), 8192, BF16) for _ in range(4)]
            self.ringptr = 0
            self.ring_content = [None] * 4
            self.nconv = 0
            self.xoff = alloc(32768)
            self.x = Buf(A, self.xoff, 8192, F32)
            self.SL = alloc(9 * 8192)
            assert off <= ARENA, off
            SL = self.SL

            def slot(i, nbytes_off=0):
                return SL + 8192 * i + nbytes_off
            self.hT = Buf(A, slot(0), 8192, BF16)
            self.h2T = self.hT
            self.xs1 = Buf(A, slot(5), 8192, BF16)
            self.xs2 = Buf(A, slot(7), 8192, BF16)
            self.qT = Buf(A, slot(2), 4096, BF16)
            self.suT = Buf(A, slot(3), 4096, BF16)
            self.sv32 = Buf(A, slot(5), 4096, F32)
            self.svn = Buf(A, slot(7), 4096, BF16)
            self.lng = Buf(A, slot(4), 1024, F32)
            self.lnb = Buf(A, slot(8), 1024, F32)
            self.osguT = Buf(A, slot(4), 4096, BF16)
            self.kblk = [Buf(A, slot(5, 2048 * i), 1024, BF16) for i in range(4)]
            self.vblk = [Buf(A, slot(6, 5120 * i), 8 * 258, BF16) for i in range(3)]
            self.oattnT = Buf(A, slot(3), 4096, BF16)
            self.mtmp = [Buf(A, slot(2, 2048 * i), 512, F32) for i in range(4)]
            self.mergedT = Buf(A, slot(5), 8192, BF16)
            self.uT = [Buf(A, slot(2), 8192, BF16), Buf(A, slot(4), 8192, BF16)]
            self.ost = [Buf(A, slot(2), 2048, F32), Buf(A, slot(3), 2048, F32)]
            self.gfb = Buf(A, slot(4), 2048, F32)
            self.xsP = Buf(A, slot(6), 8192, BF16)
            self.xstg = Buf(A, slot(8), 2048, F32)
            self.xstg2 = Buf(A, self.o0[0].off, 2048, F32)
            self.kst = Buf(A, slot(2), 4096, BF16)
            self.vst = Buf(A, slot(3), 4 * 4 * 258, BF16)
            self.sguw32 = Buf(A, slot(8), 1024, F32)

            self.stores = []
            self.dbg = {}
            self.load_consts()
            first, rest = self.conv_order()
            for (nm, g) in first:
                self.convert(nm, g)
            S.op("dve", lambda e: e.memset(self.vst.all()[0], 1.0), writes=self.vst.all()[1])
            ntile1 = self.NP // T + self.SS // T
            per = (len(rest) + ntile1 - 1) // ntile1
            ti = 0
            phase1 = [(self.I["xp"], t) for t in range(self.NP // T)] + [(self.I["xsf"], t) for t in range(self.SS // T)]
            for sq, xsrc, rsrc, n in (("p", self.I["xp"], self.I["ropeP"], self.NP // T), ("s", self.I["xsf"], self.I["ropeSF"], self.SS // T)):
                for t in range(n):
                    if ti == 0:
                        self.fe_a(xsrc[0:T, :], self.xs1, True)
                    self.fe_b(self.xs1, self.c_g1)
                    nxt = phase1[ti + 1] if ti + 1 < len(phase1) else None
                    if nxt is not None:
                        self.fe_a(nxt[0][nxt[1] * T:(nxt[1] + 1) * T, :], self.xs1, True)
                    self.kv_tile(sq, xsrc, rsrc, t)
                    for (nm, g) in rest[ti * per:(ti + 1) * per]:
                        self.convert(nm, g)
                    ti += 1
            for (nm, g) in rest[ti * per:]:
                self.convert(nm, g)
            tiles2 = ([("p", self.I["xp"], self.I["ropeP"], self.yp, t, self.NP) for t in range(self.NP // T)] +
                      [("s", self.I["xso"], self.I["ropeSO"], self.ys, t, self.SS) for t in range(self.NS // T)])
            self.fe_a_stream(tiles2[0][1][0:T, :])
            for i2, (sq, xsrc, rsrc, ydst, t, L) in enumerate(tiles2):
                nxt = None
                if i2 + 1 < len(tiles2):
                    nx = tiles2[i2 + 1]
                    nxt = nx[1][nx[4] * T:(nx[4] + 1) * T, :]
                self.main_tile(sq, xsrc, rsrc, ydst, t, L, nxt)
            S.finalize(self.stores)
            S.emit()
        return nc

    def dump(self, name, buf, parts=128):
        import os
        if not os.environ.get("KDBG"):
            return
        if name in self.dbg:
            return
        a, r = buf.all(parts)
        d = self.nc.dram_tensor("dbg_" + name, [parts, buf.n], buf.dtype, kind="ExternalOutput").ap()
        self.dbg[name] = d
        self.stores.append(self.S.dma("pool", [(d, a)], reads=r, key=("dbg", name)))

    def nextbank(self, lo=0, hi=8):
        b = self.bankptr
        if b < lo or b >= hi:
            b = lo
        self.bankptr = b + 1 if b + 1 < hi else lo
        return b

    def load_consts(self):
        S, I = self.S, self.I
        pairs = []
        wr = []
        for buf, src, parts in [(self.c_ident, I["ident"], 128), (self.c_pswap, I["pswap"], 128),
                                (self.c_g1, I["g1"], 128), (self.c_g2, I["g2"], 128),
                                (self.c_subg, I["subg"], 128), (self.c_sgub, I["sgub"], 128),
                                (self.c_lamv, I["lamv"], 128), (self.sguw32, I["sguw"], 128)]:
            a, r = buf.all()
            pairs.append((a, src))
            wr += r
        S.dma("sp", pairs, writes=wr, key="c")
        a32, r32 = self.sguw32.all()
        ab, rb = self.c_sguw.all()
        S.op("dve", lambda e: e.tensor_copy(out=ab, in_=a32), reads=r32, writes=rb)
        la, lr = self.c_lamv.all()
        ja, jr = self.rt[1].sub(0, 256)
        st, sr = self.c_stat.all()
        S.op("dve", lambda e: e.tensor_tensor(out=self.rt[0].sub(0, 128)[0], in0=la[:, 0:128], in1=la[:, 128:256], op=ALU.mult),
             reads=lr, writes=self.rt[0].sub(0, 128)[1])
        S.op("dve", lambda e: e.tensor_tensor(out=self.rt[0].sub(128, 128)[0], in0=la[:, 256:384], in1=la[:, 384:512], op=ALU.mult),
             reads=lr, writes=self.rt[0].sub(128, 128)[1])
        r0a, r0r = self.rt[0].sub(0, 256)
        S.op("act", lambda e: e.activation(out=ja[:, 0:128], in_=r0a[:, 0:128], func=AF.Copy, accum_out=st[:, 0:1]),
             reads=r0r, writes=jr + sr)
        S.op("act", lambda e: e.activation(out=ja[:, 128:256], in_=r0a[:, 128:256], func=AF.Copy, accum_out=st[:, 1:2]),
             reads=r0r, writes=jr + sr)
        S.op("act", lambda e: e.activation(out=st[:, 2:4], in_=st[:, 0:2], func=AF.Exp), reads=sr, writes=sr)
        S.op("dve", lambda e: e.tensor_tensor(out=st[:, 4:5], in0=st[:, 3:4], in1=st[:, 2:3], op=ALU.subtract), reads=sr, writes=sr)
        S.op("dve", lambda e: e.tensor_scalar(out=st[:, 5:6], in0=st[:, 4:5], scalar1=-LAMBDA_INIT, scalar2=None, op0=ALU.add),
             reads=sr, writes=sr)

    def conv_order(self):
        first = [("in", g) for g in (2, 3, 4, 5)]
        rest = ([("in", g) for g in (8, 9, 0, 1, 6, 7)] + [x for g in range(4) for x in (("in", 10 + g), ("in", 14 + g), ("bra", g), ("brs", g))] +
                [("out", g) for g in range(4)])
        for s_ in range(4):
            rest += [("up", s_ * 4 + g) for g in range(4)]
        for s_ in range(4):
            rest += [("down", cg * 4 + s_) for cg in range(4)]
        return first, rest

    def convert(self, nm, g):
        S, I = self.S, self.I
        srcs = {"in": I["w_in"], "bra": I["w_bra"], "brs": I["w_brs"], "out": I["w_out"], "up": I["w_up"], "down": I["w_down"]}
        W = srcs[nm]
        kc = 8 if nm in ("bra", "brs") else 16
        if nm == "down":
            cg, s_ = g // 4, g % 4
            src = W[s_ * 2048:(s_ + 1) * 2048, cg * 512:(cg + 1) * 512]
        else:
            src = W[:, g * 512:(g + 1) * 512]
        src = src.rearrange("(k p) j -> p k j", p=128)
        i = self.nconv
        self.nconv += 1
        S.dma("pool", [(self.wscr[(nm, g)].rearrange("p (k j) -> p k j", k=kc), src)], writes=[self.wres[(nm, g)]], key=("cv", i % 8))

    def panel(self, nm, g):
        S = self.S
        for sl, key in enumerate(self.ring_content):
            if key == (nm, g):
                return self.ring[sl]
        slot = self.ringptr
        self.ringptr = (slot + 1) % len(self.ring)
        self.ring_content[slot] = (nm, g)
        buf = self.ring[slot]
        kc = 8 if nm in ("bra", "brs") else 16
        a, r = buf.sub(0, kc * 512)
        S.dma("sp", [(a, self.wscr[(nm, g)])], reads=[self.wres[(nm, g)]], writes=r, key=("ring", slot))
        return buf

    def panel2(self, nm1, nm2, g):
        S = self.S
        slot = self.ringptr
        self.ringptr = (slot + 1) % len(self.ring)
        self.ring_content[slot] = (nm1, nm2, g)
        buf = self.ring[slot]
        a1, r1 = buf.sub(0, 4096)
        a2, r2 = buf.sub(4096, 4096)
        S.dma("sp", [(a1, self.wscr[(nm1, g)]), (a2, self.wscr[(nm2, g)])],
              reads=[self.wres[(nm1, g)], self.wres[(nm2, g)]], writes=r1 + r2, key=("ring", slot))
        return buf

    def fe_a(self, xsrc_rows, xs, load, sb=8):
        S = self.S
        if load:
            for tb in range(4):
                xta, xtr = self.x.sub(tb * D, D)
                S.dma("sp", [(xta, xsrc_rows[tb * 128:(tb + 1) * 128, :])], writes=xtr, key=("x", tb))
        st, sr = self.c_stat.all()
        for tb in range(4):
            xta, xtr = self.x.sub(tb * D, D)
            xsa, xsr = xs.sub(tb * D, D)
            S.op("act", lambda e, xta=xta, xsa=xsa, tb=tb: e.activation(out=xsa, in_=xta, func=AF.Square, accum_out=st[:, sb + tb:sb + 1 + tb]),
                 reads=xtr, writes=xsr + sr)
        S.op("act", lambda e: e.activation(out=st[:, sb + 4:sb + 8], in_=st[:, sb:sb + 4], func=AF.Sqrt, scale=1.0 / D, bias=RMS_EPS), reads=sr, writes=sr)
        S.op("dve", lambda e: e.reciprocal(out=st[:, sb + 8:sb + 12], in_=st[:, sb + 4:sb + 8]), reads=sr, writes=sr)
        for tb in range(4):
            xta, xtr = self.x.sub(tb * D, D)
            xsa, xsr = xs.sub(tb * D, D)
            if tb % 2 == 0:
                S.op("act", lambda e, xta=xta, xsa=xsa, tb=tb: e.activation(out=xsa, in_=xta, func=AF.Copy, scale=st[:, sb + 8 + tb:sb + 9 + tb]),
                     reads=xtr + sr, writes=xsr)
            else:
                S.op("dve", lambda e, xta=xta, xsa=xsa, tb=tb: e.tensor_scalar(out=xsa, in0=xta, scalar1=st[:, sb + 8 + tb:sb + 9 + tb], scalar2=None, op0=ALU.mult),
                     reads=xtr + sr, writes=xsr)

    def fe_a_stream(self, xsrc_rows, tbs=(0, 1, 2, 3)):
        S = self.S
        st, sr = self.c_stat.all()
        for tb in tbs:
            ga, gr = (self.xstg if tb % 2 == 0 else self.xstg2).all()
            S.dma("sp", [(ga, xsrc_rows[tb * 128:(tb + 1) * 128, :])], writes=gr, key=("xstg", tb % 2))
            xsa, xsr = self.xsP.sub(tb * D, D)
            S.op("act", lambda e, xsa=xsa, tb=tb, ga=ga: e.activation(out=xsa, in_=ga, func=AF.Square, accum_out=st[:, 64 + tb:65 + tb]),
                 reads=gr, writes=xsr + sr)
            S.op("act", lambda e, tb=tb: e.activation(out=st[:, 68 + tb:69 + tb], in_=st[:, 64 + tb:65 + tb], func=AF.Sqrt, scale=1.0 / D, bias=RMS_EPS), reads=sr, writes=sr)
            S.op("dve", lambda e, tb=tb: e.reciprocal(out=st[:, 72 + tb:73 + tb], in_=st[:, 68 + tb:69 + tb]), reads=sr, writes=sr)
            if tb % 2 == 0:
                S.op("act", lambda e, xsa=xsa, tb=tb, ga=ga: e.activation(out=xsa, in_=ga, func=AF.Copy, scale=st[:, 72 + tb:73 + tb]),
                     reads=gr + sr, writes=xsr)
            else:
                S.op("dve", lambda e, xsa=xsa, tb=tb, ga=ga: e.tensor_scalar(out=xsa, in0=ga, scalar1=st[:, 72 + tb:73 + tb], scalar2=None, op0=ALU.mult),
                     reads=gr + sr, writes=xsr)

    def fe_b(self, xs, gbuf):
        S = self.S
        ia, ir = self.c_ident.all()
        ga, gr = gbuf.all()
        for pr in range(8):
            b = self.nextbank()
            pb = self.banks[b][:].bitcast(BF16)
            for c in range(2):
                cc = pr * 2 + c
                for tb in range(4):
                    xsa, xsr = xs.sub(tb * D + cc * 128, 128)
                    S.op("pe", lambda e, pb=pb, c=c, tb=tb, xsa=xsa: e.transpose(out=pb[:, c * 512 + tb * 128:c * 512 + (tb + 1) * 128], in_=xsa, identity=ia),
                         reads=xsr + ir, writes=[self.bres[b]])
            for c in range(2):
                cc = pr * 2 + c
                ha, hr = self.hT.sub(cc * 512, 512)
                if pr % 2 == 0:
                    S.op("dve", lambda e, pb=pb, c=c, cc=cc, ha=ha: e.tensor_scalar(out=ha, in0=pb[:, c * 512:(c + 1) * 512], scalar1=ga[:, cc:cc + 1], scalar2=None, op0=ALU.mult),
                         reads=[self.bres[b]] + gr, writes=hr)
                else:
                    S.op("act", lambda e, pb=pb, c=c, cc=cc, ha=ha: e.activation(out=ha, in_=pb[:, c * 512:(c + 1) * 512], func=AF.Copy, scale=ga[:, cc:cc + 1]),
                         reads=[self.bres[b]] + gr, writes=hr)

    def front_end(self, xsrc_rows, xs, gbuf, load):
        self.fe_a(xsrc_rows, xs, load)
        self.fe_b(xs, gbuf)

    def mm_fm(self, b, pbuf, j, act, KC, base=0):
        S = self.S
        ba = self.banks[b][:]
        for kc in range(KC):
            pa, pr = pbuf.sub(base + kc * 512 + j * 128, 128)
            aa, ar = act.sub(kc * 512, 512)
            S.op("pe", lambda e, pa=pa, aa=aa, kc=kc: e.matmul(ba, lhsT=pa, rhs=aa, start=(kc == 0), stop=(kc == KC - 1)),
                 reads=pr + ar, writes=[self.bres[b]])

    def mm_tm(self, b, act, tb, pbuf, KC):
        S = self.S
        ba = self.banks[b][:]
        for kc in range(KC):
            pa, pr = pbuf.sub(kc * 512, 512)
            aa, ar = act.sub(kc * 512 + tb * 128, 128)
            S.op("pe", lambda e, pa=pa, aa=aa, kc=kc: e.matmul(ba, lhsT=aa, rhs=pa, start=(kc == 0), stop=(kc == KC - 1)),
                 reads=pr + ar, writes=[self.bres[b]])

    def load_rope(self, rope_src, t0):
        S = self.S
        ra, rr = self.rope.all(parts=32)
        S.dma("sp", [(ra.rearrange("p (a t) -> p a t", a=2), rope_src[:, :, t0:t0 + T])], writes=rr, key="rope")

    def rotary_proj(self, grp, dst, evac_i, hook=None):
        S = self.S
        ra, rr = self.rope.all(parts=32)
        pw, pwr = self.c_pswap.all()

        def do_rot(hc, b, da, dr):
            ba = self.banks[b][:]
            b2 = self.nextbank()
            b2a = self.banks[b2][:]
            S.op("pe", lambda e, b2a=b2a, da=da: e.matmul(b2a, lhsT=pw, rhs=da, start=True, stop=True),
                 reads=pwr + dr, writes=[self.bres[b2]])
            t1a, t1r = self.rt[0].sub(0, 512, parts=32)
            t2a, t2r = self.rt[1].sub(0, 512, parts=32)
            S.op("dve", lambda e, t1a=t1a, b2a=b2a: e.tensor_tensor(out=t1a, in0=b2a[0:32, :], in1=ra[:, 512:1024], op=ALU.mult),
                 reads=[self.bres[b2]] + rr, writes=t1r)
            S.op("dve", lambda e, t2a=t2a, ba=ba: e.tensor_tensor(out=t2a, in0=ba[0:32, :], in1=ra[:, 0:512], op=ALU.mult),
                 reads=[self.bres[b]] + rr, writes=t2r)
            d32, d32r = dst.sub(hc * 512, 512, parts=32)
            S.op("pool", lambda e, d32=d32, t1a=t1a, t2a=t2a: e.tensor_tensor(out=d32, in0=t1a, in1=t2a, op=ALU.add),
                 reads=t1r + t2r, writes=d32r)

        pending = None
        for hc in range(8):
            if hc % 4 == 0:
                pbuf = self.panel("in", grp * 2 + hc // 4)
            b = self.nextbank()
            self.mm_fm(b, pbuf, hc % 4, self.hT, 16)
            ba = self.banks[b][:]
            da, dr = dst.sub(hc * 512, 512)
            if (hc + evac_i) % 2 == 0:
                S.op("act", lambda e, da=da, ba=ba: e.copy(out=da, in_=ba), reads=[self.bres[b]], writes=dr)
            else:
                S.op("dve", lambda e, da=da, ba=ba: e.tensor_copy(out=da, in_=ba), reads=[self.bres[b]], writes=dr)
            if pending is not None:
                do_rot(*pending)
            pending = (hc, b, da, dr)
            if hook is not None:
                hook(hc)
        do_rot(*pending)

    def kv_tile(self, sq, xsrc, rope_src, t):
        S = self.S
        t0 = t * T
        self.load_rope(rope_src, t0)
        self.rotary_proj(1, self.kst, 0)
        pairs = []
        ka, kr = self.kst.all()
        for h in range(4):
            pairs.append((self.kscr[sq][h, :, :, t0:t0 + T], ka[:, h * 1024:(h + 1) * 1024].rearrange("p (c t) -> p c t", c=2)))
        S.dma("pool", pairs, reads=kr, writes=[self.kvres[sq]], key="kst")
        self.dump("kst", self.kst)
        va, vr = self.vst.all()
        v4 = va.rearrange("p (h t e) -> p h t e", h=4, t=4)
        i = 0
        for cg in range(2):
            pbuf = self.panel("in", 4 + cg)
            for tb in range(4):
                b = self.nextbank()
                self.mm_tm(b, self.hT, tb, pbuf, 16)
                ba = self.banks[b][:].rearrange("p (h e) -> p h e", h=2)
                dst = v4[:, cg * 2:cg * 2 + 2, tb, 0:256]
                if i % 2 == 0:
                    S.op("act", lambda e, dst=dst, ba=ba: e.copy(out=dst, in_=ba), reads=[self.bres[b]], writes=vr)
                else:
                    S.op("dve", lambda e, dst=dst, ba=ba: e.tensor_copy(out=dst, in_=ba), reads=[self.bres[b]], writes=vr)
                i += 1
        pairs = []
        for h in range(4):
            pairs.append((self.vscr[sq][h, :, t * 4:t * 4 + 4, :].rearrange("p t e -> p (t e)"), va[:, h * 1032:(h + 1) * 1032]))
        S.dma("pool", pairs, reads=vr, writes=[self.kvres[sq]], key="vst")
        self.dump("vst", self.vst)

    def main_tile(self, sq, xsrc, rope_src, ydst, t, L, nxt):
        S = self.S
        t0 = t * T
        st, sr = self.c_stat.all()
        self.load_rope(rope_src, t0)
        for tb in range(4):
            xta, xtr = self.x.sub(tb * D, D)
            S.dma("pool", [(xta, xsrc[t0 + tb * 128:t0 + (tb + 1) * 128, :])], writes=xtr, key=("xr", tb))
        self.fe_b(self.xsP, self.c_g1)
        for cg in range(2):
            pbuf = self.panel("in", 8 + cg)
            for tb in range(4):
                b = self.nextbank()
                self.mm_tm(b, self.hT, tb, pbuf, 16)
                ba = self.banks[b][:]
                da, dr = self.sv32.sub(tb * 1024 + cg * 512, 512)
                S.op("act", lambda e, da=da, ba=ba: e.activation(out=da, in_=ba, func=AF.Gelu), reads=[self.bres[b]], writes=dr)
        S.dma("sp", [(self.lng.all()[0], self.I["lng"]), (self.lnb.all()[0], self.I["lnb"])],
              writes=self.lng.all()[1] + self.lnb.all()[1], key="ln")
        lga, lgr = self.lng.all()
        lba, lbr = self.lnb.all()
        def ln_step(tb):
            sva, svr = self.sv32.sub(tb * 1024, 1024)
            S.op("dve", lambda e, sva=sva: e.bn_stats(out=st[:, 24:30], in_=sva[:, 0:512]), reads=svr, writes=sr)
            S.op("dve", lambda e, sva=sva: e.bn_stats(out=st[:, 30:36], in_=sva[:, 512:1024]), reads=svr, writes=sr)
            S.op("dve", lambda e: e.bn_aggr(out=st[:, 36:38], in_=st[:, 24:36]), reads=sr, writes=sr)
            S.op("act", lambda e: e.activation(out=st[:, 38:39], in_=st[:, 37:38], func=AF.Sqrt, scale=1.0, bias=LN_EPS), reads=sr, writes=sr)
            S.op("dve", lambda e: e.reciprocal(out=st[:, 39:40], in_=st[:, 38:39]), reads=sr, writes=sr)
            S.op("dve", lambda e, sva=sva: e.tensor_scalar(out=sva, in0=sva, scalar1=st[:, 36:37], scalar2=st[:, 39:40], op0=ALU.subtract, op1=ALU.mult),
                 reads=svr + sr, writes=svr)
            S.op("pool", lambda e, sva=sva: e.tensor_tensor(out=sva, in0=sva, in1=lga, op=ALU.mult), reads=svr + lgr, writes=svr)
            sna, snr = self.svn.sub(tb * 1024, 1024)
            S.op("pool", lambda e, sva=sva, sna=sna: e.tensor_tensor(out=sna, in0=sva, in1=lba, op=ALU.add), reads=svr + lbr, writes=snr)
        self.rotary_proj(0, self.qT, 1, hook=lambda hc: ln_step(hc // 2) if hc % 2 == 1 else None)
        self.dump("hT", self.hT)
        self.dump("qT", self.qT)
        for c in range(8):
            if c % 4 == 0:
                pbuf = self.panel("in", 6 + c // 4)
            b = self.nextbank()
            self.mm_fm(b, pbuf, c % 4, self.hT, 16)
            ba = self.banks[b][:]
            da, dr = self.suT.sub(c * 512, 512)
            S.op("act", lambda e, da=da, ba=ba: e.activation(out=da, in_=ba, func=AF.Gelu), reads=[self.bres[b]], writes=dr)
        wa, wr = self.c_sguw.all()
        sba, sbr = self.c_sgub.all()
        for tb in range(4):
            for half in range(2):
                b = self.nextbank()
                ba = self.banks[b][:]
                for gi in range(4):
                    g = half * 4 + gi
                    sna, snr = self.svn.sub(tb * 1024 + g * 128, 128)
                    S.op("pe", lambda e, ba=ba, gi=gi, g=g, sna=sna: e.matmul(ba[:, gi * 128:(gi + 1) * 128], lhsT=sna, rhs=wa[:, g * 128:(g + 1) * 128], start=True, stop=True),
                         reads=snr + wr, writes=[self.bres[b]])
                t1a, t1r = self.rt[half].all()
                S.op("dve", lambda e, t1a=t1a, ba=ba, half=half: e.tensor_tensor(out=t1a, in0=ba, in1=sba[:, half * 512:(half + 1) * 512], op=ALU.add),
                     reads=[self.bres[b]] + sbr, writes=t1r)
                oa, orr = self.osguT.all()
                sua, sur = self.suT.all()
                o3 = oa.rearrange("p (g t) -> p g t", g=8)[:, half * 4:half * 4 + 4, tb * 128:(tb + 1) * 128]
                s3 = sua.rearrange("p (g t) -> p g t", g=8)[:, half * 4:half * 4 + 4, tb * 128:(tb + 1) * 128]
                t3 = t1a.rearrange("p (g t) -> p g t", g=4)
                S.op("pool", lambda e, o3=o3, s3=s3, t3=t3: e.tensor_tensor(out=o3, in0=t3, in1=s3, op=ALU.mult),
                     reads=t1r + sur, writes=orr)
        self.dump("suT", self.suT)
        self.dump("svn", self.svn)
        self.dump("osguT", self.osguT)
        pend = self.attention_tile(sq, L)
        for c in range(16):
            if c % 4 == 0:
                p_ga = self.panel("in", 10 + c // 4)
                p_gs = self.panel("in", 14 + c // 4)
                p_ab = self.panel2("bra", "brs", c // 4)
            bG = self.nextbank()
            self.mm_fm(bG, p_ga, c % 4, self.hT, 16)
            bS = self.nextbank()
            self.mm_fm(bS, p_gs, c % 4, self.hT, 16)
            if c == 0:
                pend()
                self.dump("oattnT", self.oattnT)
            bB = self.nextbank()
            self.mm_fm(bB, p_ab, c % 4, self.osguT, 8, base=4096)
            bA = self.nextbank()
            self.mm_fm(bA, p_ab, c % 4, self.oattnT, 8)
            sa, sar = self.mtmp[0].all()
            ss, ssr = self.mtmp[1].all()
            m1, m1r = self.mtmp[2].all()
            m2, m2r = self.mtmp[3].all()
            S.op("act", lambda e, sa=sa, bG=bG: e.activation(out=sa, in_=self.banks[bG][:], func=AF.Sigmoid), reads=[self.bres[bG]], writes=sar)
            S.op("act", lambda e, ss=ss, bS=bS: e.activation(out=ss, in_=self.banks[bS][:], func=AF.Sigmoid), reads=[self.bres[bS]], writes=ssr)
            S.op("dve", lambda e, m1=m1, sa=sa, bA=bA: e.tensor_tensor(out=m1, in0=self.banks[bA][:], in1=sa, op=ALU.mult), reads=[self.bres[bA]] + sar, writes=m1r)
            S.op("dve", lambda e, m2=m2, ss=ss, bB=bB: e.tensor_tensor(out=m2, in0=self.banks[bB][:], in1=ss, op=ALU.mult), reads=[self.bres[bB]] + ssr, writes=m2r)
            ma, mr = self.mergedT.sub(c * 512, 512)
            S.op("pool", lambda e, ma=ma, m1=m1, m2=m2: e.tensor_tensor(out=ma, in0=m1, in1=m2, op=ALU.add), reads=m1r + m2r, writes=mr)
        for cg in range(4):
            pbuf = self.panel("out", cg)
            for tb in range(4):
                b = self.nextbank()
                self.mm_tm(b, self.mergedT, tb, pbuf, 16)
                xa, xr = self.x.sub(tb * D + cg * 512, 512)
                S.op("dve", lambda e, xa=xa, b=b: e.tensor_tensor(out=xa, in0=self.banks[b][:], in1=xa, op=ALU.add), reads=[self.bres[b]] + xr, writes=xr)
        self.dump("mergedT", self.mergedT)
        self.dump("x1", self.x)
        self.front_end(None, self.xs2, self.c_g2, False)
        self.dump("h2T", self.h2T)
        seq = ["u0", "u1", "d0", "u2", "d1", "u3", "d2", "d3"]
        ei = 0
        for item in seq:
            if item == "d0" and nxt is not None:
                self.fe_a_stream(nxt, (0, 1))
            if item == "d1" and nxt is not None:
                self.fe_a_stream(nxt, (2, 3))
            s = int(item[1])
            u = self.uT[s % 2]
            if item[0] == "u":
                for g in range(4):
                    pbuf = self.panel("up", s * 4 + g)
                    for j in range(4):
                        b = self.nextbank()
                        self.mm_fm(b, pbuf, j, self.h2T, 16)
                        ra, rr = self.relu[ei % 2].all()
                        ei += 1
                        S.op("act", lambda e, ra=ra, b=b: e.activation(out=ra, in_=self.banks[b][:], func=AF.Relu), reads=[self.bres[b]], writes=rr)
                        ua, ur = u.sub((g * 4 + j) * 512, 512)
                        S.op("pool", lambda e, ua=ua, ra=ra: e.tensor_tensor(out=ua, in0=ra, in1=ra, op=ALU.mult), reads=rr, writes=ur)
            else:
                for cg in range(4):
                    pbuf = self.panel("down", cg * 4 + s)
                    for tb in range(4):
                        b = self.nextbank()
                        self.mm_tm(b, u, tb, pbuf, 16)
                        xa, xr = self.x.sub(tb * D + cg * 512, 512)
                        S.op("dve", lambda e, xa=xa, b=b: e.tensor_tensor(out=xa, in0=self.banks[b][:], in1=xa, op=ALU.add), reads=[self.bres[b]] + xr, writes=xr)
        self.dump("x2", self.x)
        ga, gr = self.gfb.all()
        S.dma("pool", [(ga, self.I["gf"])], writes=gr, key="gf")
        for tb in range(4):
            xta, xtr = self.x.sub(tb * D, D)
            ja, jr = self.ost[tb % 2].all()
            S.op("act", lambda e, xta=xta, tb=tb, ja=ja: e.activation(out=ja, in_=xta, func=AF.Square, accum_out=st[:, 8 + tb:9 + tb]),
                 reads=xtr, writes=jr + sr)
        S.op("act", lambda e: e.activation(out=st[:, 12:16], in_=st[:, 8:12], func=AF.Sqrt, scale=1.0 / D, bias=RMS_EPS), reads=sr, writes=sr)
        S.op("dve", lambda e: e.reciprocal(out=st[:, 16:20], in_=st[:, 12:16]), reads=sr, writes=sr)
        for tb in range(4):
            xta, xtr = self.x.sub(tb * D, D)
            oa, orr = self.ost[tb % 2].all()
            S.op("dve", lambda e, xta=xta, oa=oa, tb=tb: e.scalar_tensor_tensor(out=oa, in0=xta, scalar=st[:, 16 + tb:17 + tb], in1=ga, op0=ALU.mult, op1=ALU.mult),
                 reads=xtr + sr + gr, writes=orr)
            self.stores.append(S.dma("pool", [(ydst[t0 + tb * 128:t0 + (tb + 1) * 128, :], oa)], reads=orr, key=("ost", tb % 2)))

    def attention_tile(self, sq, L):
        S = self.S
        st, sr = self.c_stat.all()
        nch = L // 128
        cpb = min(8, nch)
        kbs = cpb * 128
        ia, ir = self.c_ident.all()
        ona, onr = self.on.all()
        sga, sgr = self.c_subg.all()
        seq = [(h, c, i) for h in range(4) for c in range(2) for i in range(nch)]
        n = len(seq)
        kviews = {}
        vviews = {}

        def getk(h, c, blk):
            if (h, c, blk) not in kviews:
                kb = self.kblk[self.kptr % 4]
                self.kptr += 1
                a, r = kb.all()
                S.dma("sp", [(a[:, 0:kbs], self.kscr[sq][h, :, c, blk * kbs:(blk + 1) * kbs])], reads=[self.kvres[sq]], writes=r, key=("kb", (self.kptr - 1) % 4))
                kviews[(h, c, blk)] = kb
            return kviews[(h, c, blk)]

        def getv(h, c, blk):
            if (h, c, blk) not in vviews:
                vb = self.vblk[self.vptr % 3]
                self.vptr += 1
                a, r = vb.all()
                S.dma("sp", [(a[:, 0:cpb * 258], self.vscr[sq][h, :, blk * cpb:(blk + 1) * cpb, :].rearrange("p t e -> p (t e)"))], reads=[self.kvres[sq]], writes=r, key=("vb", (self.vptr - 1) % 3))
                vviews[(h, c, blk)] = vb
            return vviews[(h, c, blk)]

        def qk(g):
            h, c, i = seq[g]
            qa, qr = self.qT.sub((h * 2 + c) * 512, 512)
            kb = getk(h, c, i // cpb)
            ka, kr = kb.sub((i % cpb) * 128, 128)
            b = 4 + g % 3
            S.op("pe", lambda e, b=b, ka=ka, qa=qa: e.matmul(self.banks[b][:], lhsT=ka, rhs=qa, start=True, stop=True),
                 reads=kr + qr, writes=[self.bres[b]])

        def evac(h, c):
            o0a, o0r = self.o0[h % 2].all()
            for qb in range(4):
                acc = self.banks[qb][:]
                if c == 0:
                    S.op("dve", lambda e, acc=acc, qb=qb: e.reciprocal(out=st[:, 40 + qb:41 + qb], in_=acc[:, 256:257]), reads=[self.bres[qb]], writes=sr)
                    S.op("dve", lambda e, acc=acc, qb=qb, o0a=o0a: e.tensor_scalar(out=o0a[:, qb * 256:(qb + 1) * 256], in0=acc[:, 0:256], scalar1=st[:, 40 + qb:41 + qb], scalar2=None, op0=ALU.mult),
                         reads=[self.bres[qb]] + sr, writes=o0r)
                else:
                    S.op("dve", lambda e, acc=acc, qb=qb: e.reciprocal(out=st[:, 44 + qb:45 + qb], in_=acc[:, 256:257]), reads=[self.bres[qb]], writes=sr)
                    S.op("dve", lambda e, qb=qb: e.tensor_tensor(out=st[:, 48 + qb:49 + qb], in0=st[:, 44 + qb:45 + qb], in1=st[:, 5:6], op=ALU.mult), reads=sr, writes=sr)
                    S.op("dve", lambda e, acc=acc, qb=qb, o0a=o0a: e.scalar_tensor_tensor(out=o0a[:, qb * 256:(qb + 1) * 256], in0=acc[:, 0:256], scalar=st[:, 48 + qb:49 + qb], in1=o0a[:, qb * 256:(qb + 1) * 256], op0=ALU.mult, op1=ALU.add),
                         reads=[self.bres[qb]] + sr + o0r, writes=o0r)

        def fin_dve(h):
            o0a, o0r = self.o0[h % 2].all()
            for qb in range(4):
                S.op("act", lambda e, qb=qb, o0a=o0a: e.activation(out=ona[:, qb * 256:(qb + 1) * 256], in_=o0a[:, qb * 256:(qb + 1) * 256], func=AF.Square, accum_out=st[:, 52 + qb:53 + qb]),
                     reads=o0r, writes=onr + sr)
            S.op("act", lambda e: e.activation(out=st[:, 56:60], in_=st[:, 52:56], func=AF.Sqrt, scale=1.0 / 256, bias=RMS_EPS), reads=sr, writes=sr)
            S.op("dve", lambda e: e.reciprocal(out=st[:, 60:64], in_=st[:, 56:60]), reads=sr, writes=sr)
            for qb in range(4):
                S.op("dve", lambda e, qb=qb, o0a=o0a: e.tensor_scalar(out=ona[:, qb * 256:(qb + 1) * 256], in0=o0a[:, qb * 256:(qb + 1) * 256], scalar1=st[:, 60 + qb:61 + qb], scalar2=None, op0=ALU.mult),
                     reads=o0r + sr, writes=onr)

        def fin_pe(h):
            b = 7
            pb = self.banks[b][:].bitcast(BF16)
            for qb in range(4):
                for j in range(2):
                    S.op("pe", lambda e, qb=qb, j=j: e.transpose(out=pb[:, j * 512 + qb * 128:j * 512 + (qb + 1) * 128], in_=ona[:, qb * 256 + j * 128:qb * 256 + (j + 1) * 128], identity=ia),
                         reads=onr + ir, writes=[self.bres[b]])
            for j in range(2):
                oa, orr = self.oattnT.sub((h * 2 + j) * 512, 512)
                S.op("dve", lambda e, oa=oa, j=j: e.tensor_scalar(out=oa, in0=pb[:, j * 512:(j + 1) * 512], scalar1=sga[:, j:j + 1], scalar2=1.0 - LAMBDA_INIT, op0=ALU.mult, op1=ALU.mult),
                     reads=[self.bres[b]] + sgr, writes=orr)

        LA = 2
        for g in range(min(LA, n)):
            qk(g)
        pending_evac = None
        deferred = {}
        for g in range(n):
            h, c, i = seq[g]
            if g + LA < n:
                qk(g + LA)
            b = 4 + g % 3
            ea, er = self.E[g % 4].all()
            S.op("act", lambda e, b=b, ea=ea: e.activation(out=ea, in_=self.banks[b][:], func=AF.Exp, scale=ATTN_SCALE),
                 reads=[self.bres[b]], writes=er)
            if i == 0 and pending_evac is not None:
                evac(*pending_evac)
                pending_evac = None
            for fn in deferred.pop(g, []):
                fn()
            vb = getv(h, c, i // cpb)
            va, vr = vb.sub((i % cpb) * 258, 257)
            for qb in range(4):
                S.op("pe", lambda e, qb=qb, ea=ea, va=va, i=i: e.matmul(self.banks[qb][:, 0:257], lhsT=ea[:, qb * 128:(qb + 1) * 128], rhs=va, start=(i == 0), stop=(i == nch - 1)),
                     reads=er + vr, writes=[self.bres[qb]])
            if i == nch - 1:
                pending_evac = (h, c)
                if c == 1 and h < 3:
                    deferred.setdefault(g + 3, []).append(lambda h=h: fin_dve(h))
                    deferred.setdefault(g + 8, []).append(lambda h=h: fin_pe(h))
        evac(*pending_evac)
        for g in sorted(deferred):
            for fn in deferred[g]:
                fn()
        fin_dve(3)
        return lambda: fin_pe(3)

    kptr = 0
    vptr = 0


def _rope_tables(positions):
    rot = 32
    inv = (1.0 / (np.float32(ROPE_THETA) ** (np.arange(0, rot, 2, dtype=np.float32) / np.float32(rot)))).astype(np.float32)
    ang = positions.astype(np.float32)[None, :] * inv[:, None]
    c = np.cos(ang).astype(np.float32)
    s = np.sin(ang).astype(np.float32)
    C = np.concatenate([c, c], axis=0)
    Sg = np.concatenate([-s, s], axis=0)
    return np.ascontiguousarray(np.stack([C, Sg], axis=1))


_NC_CACHE = {}


def run(inp, NP, SS, n_prompt=8, n_sample=2):
    NS = SS // 4
    key = (NP, SS)
    if key not in _NC_CACHE:
        _NC_CACHE[key] = Builder(NP, SS).build()
    nc = _NC_CACHE[key]
    f = lambda a: np.ascontiguousarray(np.asarray(a, dtype=np.float32))
    bc = lambda v: np.ascontiguousarray(np.broadcast_to(f(v).reshape(1, -1), (128, f(v).size)))
    common = {
        "w_in": f(inp["w_in"][0]), "w_bra": f(inp["w_br_attn"][0]), "w_brs": f(inp["w_br_sgu"][0]),
        "w_out": f(inp["w_out"][0]), "w_up": f(inp["w_up"][0]), "w_down": f(inp["w_down"][0]),
        "g1": np.ascontiguousarray(f(inp["attn_norm_g"][0]).reshape(16, 128).T),
        "g2": np.ascontiguousarray(f(inp["mlp_norm_g"][0]).reshape(16, 128).T),
        "gf": bc(inp["final_norm_g"]),
        "lamv": np.ascontiguousarray(np.concatenate([bc(inp["lambda_q1"][0]), bc(inp["lambda_k1"][0]), bc(inp["lambda_q2"][0]), bc(inp["lambda_k2"][0])], axis=1)),
        "subg": np.ascontiguousarray(f(inp["subln_g"][0]).reshape(2, 128).T),
        "lng": bc(inp["sgu_ln_g"][0]), "lnb": bc(inp["sgu_ln_b"][0]),
        "sguw": np.ascontiguousarray(np.transpose(f(inp["sgu_w"][0]), (2, 0, 1)).reshape(128, 1024)),
        "sgub": bc(f(inp["sgu_b"][0]).reshape(-1)),
        "ident": np.eye(128, dtype=np.float32).astype(ml_dtypes.bfloat16),
        "ropeP": _rope_tables(np.arange(NP)),
        "ropeSF": _rope_tables(np.arange(SS)),
    }
    psw = np.zeros((128, 128), np.float32)
    for m in range(32):
        psw[(m + 16) if m < 16 else (m - 16), m] = 1.0
    common["pswap"] = psw.astype(ml_dtypes.bfloat16)
    xp = f(inp["x_prompt"])
    xs = f(inp["x_sample"])
    in_maps = []
    for core in range(8):
        si, qi = core // 4, core % 4
        if n_sample == 1:
            si = 0
        m = dict(common)
        m["xp"] = xp[core % n_prompt]
        m["xsf"] = xs[si]
        m["xso"] = np.ascontiguousarray(xs[si, qi * NS:(qi + 1) * NS])
        m["ropeSO"] = _rope_tables(np.arange(qi * NS, (qi + 1) * NS))
        in_maps.append(m)
    res = run_bass_kernel_spmd(nc, in_maps, core_ids=list(range(8)))
    global LAST_RES
    LAST_RES = res
    yp = np.stack([np.asarray(res.results[c]["yp"], dtype=np.float32) for c in range(n_prompt)], axis=0)
    ys = np.stack([np.concatenate([np.asarray(res.results[si * 4 + qi]["ys"], dtype=np.float32) for qi in range(4)], axis=0)
                   for si in range(n_sample)], axis=0)
    return yp, ys


def kernel(**inputs):
    yp, ys = run(inputs, 2048, 8192)
    return (yp, ys)
```

```python
import contextlib
import math
import numpy as np
import ml_dtypes
import concourse.bass as bass
import concourse.mybir as mybir
from concourse.bass_utils import run_bass_kernel_spmd

F32 = mybir.dt.float32
BF16 = mybir.dt.bfloat16
AF = mybir.ActivationFunctionType
ALU = mybir.AluOpType

ENGS = ("pe", "act", "dve", "pool", "sp")

D = 2048
KD = 16
DFF = 8192
T = 512
ATTN_SCALE = 128 ** -0.5
RMS_EPS = 1e-6
LN_EPS = 1e-5
LAMBDA_INIT = 0.8 - 0.6 * math.exp(-0.3 * 0)
ROPE_THETA = 500000.0


class Res:
    __slots__ = ("w", "rs", "name", "excl")

    def __init__(self, name="", excl=False):
        self.w = None
        self.rs = []
        self.name = name
        self.excl = excl


class Op:
    __slots__ = ("eng", "fn", "deps", "isdma", "key", "ninc", "sig", "needed", "waits", "idx")


class Sched:
    def __init__(self, nc, stack, n_dma_sems=60):
        self.nc = nc
        self.ops = {e: [] for e in ENGS}
        self.esem = {e: stack.enter_context(nc.semaphore("s_" + e)) for e in ENGS if e != "sp"}
        self.dma_pool = [stack.enter_context(nc.semaphore("d%d" % i)) for i in range(n_dma_sems)]
        self.dma_keys = {}
        self.nops = 0

    def _mk(self, eng, fn, reads, writes, isdma=False, key=None, ninc=1):
        op = Op()
        op.eng = eng
        op.fn = fn
        op.isdma = isdma
        op.key = key
        op.ninc = ninc
        op.sig = None
        op.needed = False
        op.waits = None
        op.idx = self.nops
        self.nops += 1
        deps = set()
        pe_plain = (eng == "pe" and not isdma)
        if any(r.excl for r in reads):
            writes = list(writes) + [r for r in reads if r.excl]
            reads = [r for r in reads if not r.excl]
        for r in reads:
            d = r.w
            if d is not None and not (pe_plain and d.eng == "pe" and not d.isdma):
                deps.add(d)
        for w in writes:
            d = w.w
            if d is not None and (d.isdma or isdma or d.eng != eng):
                deps.add(d)
            for d in w.rs:
                if d.isdma or isdma or d.eng != eng:
                    deps.add(d)
        op.deps = deps
        for d in deps:
            d.needed = True
        for r in reads:
            rs = r.rs
            if rs and rs[-1].eng == eng and not isdma and not rs[-1].isdma:
                rs[-1] = op
            else:
                rs.append(op)
        for w in writes:
            w.w = op
            w.rs = []
        self.ops[eng].append(op)
        return op

    def op(self, eng, fn, reads=(), writes=()):
        return self._mk(eng, fn, reads, writes)

    def dma(self, queue, pairs, reads=(), writes=(), key=None, **kw):
        def fn(eng, sem, pairs=pairs, kw=kw):
            for (o, i) in pairs:
                eng.dma_start(out=o, in_=i, **kw).then_inc(sem, 16)
        return self._mk(queue, fn, reads, writes, isdma=True, key=key, ninc=len(pairs))

    def finalize(self, final_wait_ops=()):
        if final_wait_ops:
            fop = Op()
            fop.eng = "sp"; fop.fn = None; fop.isdma = False; fop.key = None; fop.ninc = 0
            fop.sig = None; fop.needed = False; fop.waits = None; fop.idx = self.nops
            fop.deps = set(final_wait_ops)
            for d in fop.deps:
                d.needed = True
            self.ops["sp"].append(fop)
        cnt = {e: 0 for e in ENGS}
        keyq = {}
        for e in ENGS:
            for op in self.ops[e]:
                if op.isdma:
                    if op.key is None:
                        op.key = ("auto", op.idx)
                    assert keyq.setdefault(op.key, e) == e, "dma key used on two queues"
                    if op.key not in self.dma_keys:
                        if not self.dma_pool:
                            raise RuntimeError("out of dma semaphores")
                        self.dma_keys[op.key] = [self.dma_pool.pop(), 0]
                    ent = self.dma_keys[op.key]
                    ent[1] += 16 * op.ninc
                    op.sig = (ent[0], ent[1])
                elif op.needed:
                    cnt[e] += 1
                    op.sig = (self.esem[e], cnt[e])
        for e in ENGS:
            waited = {}
            for op in self.ops[e]:
                need = {}
                for d in op.deps:
                    sem, val = d.sig
                    k = id(sem)
                    if k not in need or need[k][1] < val:
                        need[k] = (sem, val)
                ws = []
                for k, (sem, val) in need.items():
                    if waited.get(k, 0) >= val:
                        continue
                    waited[k] = val
                    ws.append((sem, val))
                op.waits = ws
        self.counts = cnt

    def emit(self):
        nc = self.nc
        with nc.Block() as block:
            def mk(e):
                def body(eng):
                    for op in self.ops[e]:
                        for (sem, val) in op.waits:
                            eng.wait_ge(sem, val)
                        if op.fn is None:
                            continue
                        if op.isdma:
                            op.fn(eng, op.sig[0])
                        else:
                            ins = op.fn(eng)
                            if op.needed:
                                ins.then_inc(op.sig[0], 1)
                return body
            block.tensor(mk("pe"))
            block.scalar(mk("act"))
            block.vector(mk("dve"))
            block.gpsimd(mk("pool"))
            block.sync(mk("sp"))


class Arena:
    GR = 1024

    def __init__(self, nc, stack, nbytes, name="arena"):
        self.nbytes = nbytes
        self.t = stack.enter_context(nc.sbuf_tensor(name, [128, nbytes // 2], BF16))
        self.res = [Res("%s_%d" % (name, i)) for i in range((nbytes + self.GR - 1) // self.GR)]

    def view(self, off, nbytes, dtype=BF16, parts=128):
        assert off % 4 == 0 and nbytes % 2 == 0 and off + nbytes <= self.nbytes, (off, nbytes)
        ap = self.t[0:parts, off // 2:(off + nbytes) // 2]
        if dtype != BF16:
            ap = ap.bitcast(dtype)
        return ap, self.res[off // self.GR:(off + nbytes - 1) // self.GR + 1]


class Buf:
    def __init__(self, arena, off, nelem, dtype):
        self.a = arena
        self.off = off
        self.dtype = dtype
        self.es = 4 if dtype == F32 else 2
        self.n = nelem

    def sub(self, e0, n, parts=128):
        assert e0 + n <= self.n
        return self.a.view(self.off + e0 * self.es, n * self.es, self.dtype, parts)

    def all(self, parts=128):
        return self.a.view(self.off, self.n * self.es, self.dtype, parts)


PANELS = ([("in", g, 16) for g in range(18)] + [("bra", g, 8) for g in range(4)] +
          [("brs", g, 8) for g in range(4)] + [("out", g, 16) for g in range(4)] +
          [("up", g, 16) for g in range(16)] + [("down", g * 4 + s, 16) for g in range(4) for s in range(4)])


class Builder:
    def __init__(self, NP, SS):
        assert NP % T == 0 and SS % (4 * T) == 0
        self.NP, self.SS, self.NS = NP, SS, SS // 4
        nc = self.nc = bass.Bass("TRN2", target_bir_lowering=False)
        dt = nc.dram_tensor
        NS = self.NS
        I = {}
        I["xp"] = dt("xp", [NP, D], F32, kind="ExternalInput").ap()
        I["xsf"] = dt("xsf", [SS, D], F32, kind="ExternalInput").ap()
        I["xso"] = dt("xso", [NS, D], F32, kind="ExternalInput").ap()
        I["w_in"] = dt("w_in", [D, 9216], F32, kind="ExternalInput").ap()
        I["w_bra"] = dt("w_bra", [1024, D], F32, kind="ExternalInput").ap()
        I["w_brs"] = dt("w_brs", [1024, D], F32, kind="ExternalInput").ap()
        I["w_out"] = dt("w_out", [D, D], F32, kind="ExternalInput").ap()
        I["w_up"] = dt("w_up", [D, DFF], F32, kind="ExternalInput").ap()
        I["w_down"] = dt("w_down", [DFF, D], F32, kind="ExternalInput").ap()
        I["g1"] = dt("g1", [128, 16], F32, kind="ExternalInput").ap()
        I["g2"] = dt("g2", [128, 16], F32, kind="ExternalInput").ap()
        I["gf"] = dt("gf", [128, D], F32, kind="ExternalInput").ap()
        I["lamv"] = dt("lamv", [128, 512], F32, kind="ExternalInput").ap()
        I["subg"] = dt("subg", [128, 2], F32, kind="ExternalInput").ap()
        I["lng"] = dt("lng", [128, 1024], F32, kind="ExternalInput").ap()
        I["lnb"] = dt("lnb", [128, 1024], F32, kind="ExternalInput").ap()
        I["sguw"] = dt("sguw", [128, 1024], F32, kind="ExternalInput").ap()
        I["sgub"] = dt("sgub", [128, 1024], F32, kind="ExternalInput").ap()
        I["ident"] = dt("ident", [128, 128], BF16, kind="ExternalInput").ap()
        I["pswap"] = dt("pswap", [128, 128], BF16, kind="ExternalInput").ap()
        I["ropeP"] = dt("ropeP", [32, 2, NP], F32, kind="ExternalInput").ap()
        I["ropeSF"] = dt("ropeSF", [32, 2, SS], F32, kind="ExternalInput").ap()
        I["ropeSO"] = dt("ropeSO", [32, 2, NS], F32, kind="ExternalInput").ap()
        self.I = I
        self.yp = dt("yp", [NP, D], F32, kind="ExternalOutput").ap()
        self.ys = dt("ys", [NS, D], F32, kind="ExternalOutput").ap()
        self.wscr = {}
        self.wres = {}
        for (nm, g, kc) in PANELS:
            self.wscr[(nm, g)] = dt("ws_%s_%d" % (nm, g), [128, kc * 512], BF16, kind="Internal").ap()
            self.wres[(nm, g)] = Res("ws_%s_%d" % (nm, g))
        self.kscr = {"p": dt("kscr_p", [4, 128, 2, NP], BF16, kind="Internal").ap(),
                     "s": dt("kscr_s", [4, 128, 2, SS], BF16, kind="Internal").ap()}
        self.vscr = {"p": dt("vscr_p", [4, 128, NP // 128, 258], BF16, kind="Internal").ap(),
                     "s": dt("vscr_s", [4, 128, SS // 128, 258], BF16, kind="Internal").ap()}
        self.kvres = {"p": Res("kv_p"), "s": Res("kv_s")}

    def build(self):
        nc = self.nc
        with contextlib.ExitStack() as st:
            self.S = S = Sched(nc, st)
            ARENA = 204 * 1024
            self.A = A = Arena(nc, st, ARENA)
            self.banks = [st.enter_context(nc.psum_tensor("bank%d" % i, [128, 512], F32)) for i in range(8)]
            self.bres = [Res("bank%d" % i, excl=True) for i in range(8)]
            self.bankptr = 0
            off = 0

            def alloc(nbytes):
                nonlocal off
                o = off
                off += (nbytes + 63) // 64 * 64
                return o
            K1 = 1024
            self.c_ident = Buf(A, alloc(256), 128, BF16)
            self.c_pswap = Buf(A, alloc(256), 128, BF16)
            self.c_g1 = Buf(A, alloc(64), 16, F32)
            self.c_g2 = Buf(A, alloc(64), 16, F32)
            self.c_subg = Buf(A, alloc(64), 2, F32)
            off = K1
            self.c_stat = Buf(A, alloc(1024), 256, F32)
            self.c_sguw = Buf(A, alloc(2048), 1024, BF16)
            self.c_sgub = Buf(A, alloc(4096), 1024, F32)
            self.c_lamv = Buf(A, alloc(2048), 512, F32)
            self.rope = Buf(A, alloc(4096), 1024, F32)
            self.E = [Buf(A, alloc(1024), 512, BF16) for _ in range(4)]
            self.o0 = [Buf(A, alloc(4096), 1024, F32) for _ in range(2)]
            self.on = Buf(A, alloc(2048), 1024, BF16)
            self.rt = [Buf(A, alloc(2048), 512, F32) for _ in range(2)]
            self.relu = [Buf(A, alloc(1024), 512, BF16) for _ in range(2)]
            self.ring = [Buf(A, alloc(16384), 8192, BF16) for _ in range(4)]
            self.ringptr = 0
            self.ring_content = [None] * 4
            self.nconv = 0
            self.xoff = alloc(32768)
            self.x = Buf(A, self.xoff, 8192, F32)
            self.SL = alloc(9 * 8192)
            assert off <= ARENA, off
            SL = self.SL

            def slot(i, nbytes_off=0):
                return SL + 8192 * i + nbytes_off
            self.hT = Buf(A, slot(0), 8192, BF16)
            self.h2T = self.hT
            self.xs1 = Buf(A, slot(5), 8192, BF16)
            self.xs2 = Buf(A, slot(7), 8192, BF16)
            self.qT = Buf(A, slot(2), 4096, BF16)
            self.suT = Buf(A, slot(3), 4096, BF16)
            self.sv32 = Buf(A, slot(5), 4096, F32)
            self.svn = Buf(A, slot(7), 4096, BF16)
            self.lng = Buf(A, slot(4), 1024, F32)
            self.lnb = Buf(A, slot(8), 1024, F32)
            self.osguT = Buf(A, slot(4), 4096, BF16)
            self.kblk = [Buf(A, slot(5, 2048 * i), 1024, BF16) for i in range(4)]
            self.vblk = [Buf(A, slot(6, 5120 * i), 8 * 258, BF16) for i in range(3)]
            self.oattnT = Buf(A, slot(3), 4096, BF16)
            self.mtmp = [Buf(A, slot(2, 2048 * i), 512, F32) for i in range(4)]
            self.mergedT = Buf(A, slot(5), 8192, BF16)
            self.uT = [Buf(A, slot(2), 8192, BF16), Buf(A, slot(4), 8192, BF16)]
            self.ost = [Buf(A, slot(2), 2048, F32), Buf(A, slot(3), 2048, F32)]
            self.gfb = Buf(A, self.o0[0].off, 2048, F32)
            self.xsP = Buf(A, slot(6), 8192, BF16)
            self.xstg = Buf(A, slot(8), 2048, F32)
            self.xstg2 = Buf(A, self.o0[0].off, 2048, F32)
            self.kst = Buf(A, slot(2), 4096, BF16)
            self.vst = Buf(A, slot(3), 4 * 4 * 258, BF16)
            self.sguw32 = Buf(A, slot(8), 1024, F32)

            self.stores = []
            self.dbg = {}
            self.load_consts()
            first, rest = self.conv_order()
            for (nm, g) in first:
                self.convert(nm, g)
            S.op("dve", lambda e: e.memset(self.vst.all()[0], 1.0), writes=self.vst.all()[1])
            ntile1 = self.NP // T + self.SS // T
            per = (len(rest) + ntile1 - 1) // ntile1
            ti = 0
            phase1 = [(self.I["xp"], t) for t in range(self.NP // T)] + [(self.I["xsf"], t) for t in range(self.SS // T)]
            for sq, xsrc, rsrc, n in (("p", self.I["xp"], self.I["ropeP"], self.NP // T), ("s", self.I["xsf"], self.I["ropeSF"], self.SS // T)):
                for t in range(n):
                    if ti == 0:
                        self.fe_a(xsrc[0:T, :], self.xs1, True)
                    self.fe_b(self.xs1, self.c_g1)
                    nxt = phase1[ti + 1] if ti + 1 < len(phase1) else None
                    mid = None
                    if nxt is not None:
                        mid = (lambda nxt=nxt: self.fe_a(nxt[0][nxt[1] * T:(nxt[1] + 1) * T, :], self.xs1, True))
                    self.kv_tile(sq, xsrc, rsrc, t, mid)
                    for (nm, g) in rest[ti * per:(ti + 1) * per]:
                        self.convert(nm, g)
                    ti += 1
            for (nm, g) in rest[ti * per:]:
                self.convert(nm, g)
            tiles2 = ([("p", self.I["xp"], self.I["ropeP"], self.yp, t, self.NP) for t in range(self.NP // T)] +
                      [("s", self.I["xso"], self.I["ropeSO"], self.ys, t, self.SS) for t in range(self.NS // T)])
            self.fe_a_stream(tiles2[0][1][0:T, :])
            for i2, (sq, xsrc, rsrc, ydst, t, L) in enumerate(tiles2):
                nxt = None
                if i2 + 1 < len(tiles2):
                    nx = tiles2[i2 + 1]
                    nxt = nx[1][nx[4] * T:(nx[4] + 1) * T, :]
                self.main_tile_a(rsrc, t)
                if i2 > 0:
                    pv = tiles2[i2 - 1]
                    self.final_norm(pv[3], pv[4])
                self.main_tile(sq, xsrc, rsrc, ydst, t, L, nxt)
            self.final_norm(tiles2[-1][3], tiles2[-1][4])
            S.finalize(self.stores)
            S.emit()
        return nc

    def dump(self, name, buf, parts=128):
        import os
        if not os.environ.get("KDBG"):
            return
        if name in self.dbg:
            return
        a, r = buf.all(parts)
        d = self.nc.dram_tensor("dbg_" + name, [parts, buf.n], buf.dtype, kind="ExternalOutput").ap()
        self.dbg[name] = d
        self.stores.append(self.S.dma("pool", [(d, a)], reads=r, key=("dbg", name)))

    def nextbank(self, lo=0, hi=8):
        b = self.bankptr
        if b < lo or b >= hi:
            b = lo
        self.bankptr = b + 1 if b + 1 < hi else lo
        return b

    def load_consts(self):
        S, I = self.S, self.I
        pairs = []
        wr = []
        for buf, src, parts in [(self.c_ident, I["ident"], 128), (self.c_pswap, I["pswap"], 128),
                                (self.c_g1, I["g1"], 128), (self.c_g2, I["g2"], 128),
                                (self.c_subg, I["subg"], 128), (self.c_sgub, I["sgub"], 128),
                                (self.c_lamv, I["lamv"], 128), (self.sguw32, I["sguw"], 128)]:
            a, r = buf.all()
            pairs.append((a, src))
            wr += r
        S.dma("sp", pairs, writes=wr, key="c")
        a32, r32 = self.sguw32.all()
        ab, rb = self.c_sguw.all()
        S.op("dve", lambda e: e.tensor_copy(out=ab, in_=a32), reads=r32, writes=rb)
        la, lr = self.c_lamv.all()
        ja, jr = self.rt[1].sub(0, 256)
        st, sr = self.c_stat.all()
        S.op("dve", lambda e: e.tensor_tensor(out=self.rt[0].sub(0, 128)[0], in0=la[:, 0:128], in1=la[:, 128:256], op=ALU.mult),
             reads=lr, writes=self.rt[0].sub(0, 128)[1])
        S.op("dve", lambda e: e.tensor_tensor(out=self.rt[0].sub(128, 128)[0], in0=la[:, 256:384], in1=la[:, 384:512], op=ALU.mult),
             reads=lr, writes=self.rt[0].sub(128, 128)[1])
        r0a, r0r = self.rt[0].sub(0, 256)
        S.op("act", lambda e: e.activation(out=ja[:, 0:128], in_=r0a[:, 0:128], func=AF.Copy, accum_out=st[:, 0:1]),
             reads=r0r, writes=jr + sr)
        S.op("act", lambda e: e.activation(out=ja[:, 128:256], in_=r0a[:, 128:256], func=AF.Copy, accum_out=st[:, 1:2]),
             reads=r0r, writes=jr + sr)
        S.op("act", lambda e: e.activation(out=st[:, 2:4], in_=st[:, 0:2], func=AF.Exp), reads=sr, writes=sr)
        S.op("dve", lambda e: e.tensor_tensor(out=st[:, 4:5], in0=st[:, 3:4], in1=st[:, 2:3], op=ALU.subtract), reads=sr, writes=sr)
        S.op("dve", lambda e: e.tensor_scalar(out=st[:, 5:6], in0=st[:, 4:5], scalar1=-LAMBDA_INIT, scalar2=None, op0=ALU.add),
             reads=sr, writes=sr)

    def conv_order(self):
        first = [("in", g) for g in (2, 3, 4, 5)]
        rest = ([("in", g) for g in (8, 9, 0, 1, 6, 7)] + [x for g in range(4) for x in (("in", 10 + g), ("in", 14 + g), ("bra", g), ("brs", g))] +
                [("out", g) for g in range(4)])
        for s_ in range(4):
            rest += [("up", s_ * 4 + g) for g in range(4)]
        for s_ in range(4):
            rest += [("down", cg * 4 + s_) for cg in range(4)]
        return first, rest

    def convert(self, nm, g):
        S, I = self.S, self.I
        srcs = {"in": I["w_in"], "bra": I["w_bra"], "brs": I["w_brs"], "out": I["w_out"], "up": I["w_up"], "down": I["w_down"]}
        W = srcs[nm]
        kc = 8 if nm in ("bra", "brs") else 16
        if nm == "down":
            cg, s_ = g // 4, g % 4
            src = W[s_ * 2048:(s_ + 1) * 2048, cg * 512:(cg + 1) * 512]
        else:
            src = W[:, g * 512:(g + 1) * 512]
        src = src.rearrange("(k p) j -> p k j", p=128)
        i = self.nconv
        self.nconv += 1
        S.dma("pool", [(self.wscr[(nm, g)].rearrange("p (k j) -> p k j", k=kc), src)], writes=[self.wres[(nm, g)]], key=("cv", i % 8))

    def panel(self, nm, g):
        S = self.S
        for sl, key in enumerate(self.ring_content):
            if key == (nm, g):
                return self.ring[sl]
        slot = self.ringptr
        self.ringptr = (slot + 1) % len(self.ring)
        self.ring_content[slot] = (nm, g)
        buf = self.ring[slot]
        kc = 8 if nm in ("bra", "brs") else 16
        a, r = buf.sub(0, kc * 512)
        S.dma("sp", [(a, self.wscr[(nm, g)])], reads=[self.wres[(nm, g)]], writes=r, key=("ring", slot))
        return buf

    def panel2(self, nm1, nm2, g):
        S = self.S
        slot = self.ringptr
        self.ringptr = (slot + 1) % len(self.ring)
        self.ring_content[slot] = (nm1, nm2, g)
        buf = self.ring[slot]
        a1, r1 = buf.sub(0, 4096)
        a2, r2 = buf.sub(4096, 4096)
        S.dma("sp", [(a1, self.wscr[(nm1, g)]), (a2, self.wscr[(nm2, g)])],
              reads=[self.wres[(nm1, g)], self.wres[(nm2, g)]], writes=r1 + r2, key=("ring", slot))
        return buf

    def fe_a(self, xsrc_rows, xs, load, sb=8):
        S = self.S
        if load:
            for tb in range(4):
                xta, xtr = self.x.sub(tb * D, D)
                S.dma("sp", [(xta, xsrc_rows[tb * 128:(tb + 1) * 128, :])], writes=xtr, key=("x", tb))
        st, sr = self.c_stat.all()
        for tb in range(4):
            xta, xtr = self.x.sub(tb * D, D)
            xsa, xsr = xs.sub(tb * D, D)
            S.op("act", lambda e, xta=xta, xsa=xsa, tb=tb: e.activation(out=xsa, in_=xta, func=AF.Square, accum_out=st[:, sb + tb:sb + 1 + tb]),
                 reads=xtr, writes=xsr + sr)
        S.op("act", lambda e: e.activation(out=st[:, sb + 4:sb + 8], in_=st[:, sb:sb + 4], func=AF.Sqrt, scale=1.0 / D, bias=RMS_EPS), reads=sr, writes=sr)
        S.op("dve", lambda e: e.reciprocal(out=st[:, sb + 8:sb + 12], in_=st[:, sb + 4:sb + 8]), reads=sr, writes=sr)
        for tb in range(4):
            xta, xtr = self.x.sub(tb * D, D)
            xsa, xsr = xs.sub(tb * D, D)
            if tb % 2 == 0:
                S.op("act", lambda e, xta=xta, xsa=xsa, tb=tb: e.activation(out=xsa, in_=xta, func=AF.Copy, scale=st[:, sb + 8 + tb:sb + 9 + tb]),
                     reads=xtr + sr, writes=xsr)
            else:
                S.op("dve", lambda e, xta=xta, xsa=xsa, tb=tb: e.tensor_scalar(out=xsa, in0=xta, scalar1=st[:, sb + 8 + tb:sb + 9 + tb], scalar2=None, op0=ALU.mult),
                     reads=xtr + sr, writes=xsr)

    def fe_a_stream(self, xsrc_rows, tbs=(0, 1, 2, 3)):
        S = self.S
        st, sr = self.c_stat.all()
        for tb in tbs:
            ga, gr = (self.xstg if tb % 2 == 0 else self.xstg2).all()
            S.dma("sp", [(ga, xsrc_rows[tb * 128:(tb + 1) * 128, :])], writes=gr, key=("xstg", tb % 2))
            xsa, xsr = self.xsP.sub(tb * D, D)
            S.op("act", lambda e, xsa=xsa, tb=tb, ga=ga: e.activation(out=xsa, in_=ga, func=AF.Square, accum_out=st[:, 64 + tb:65 + tb]),
                 reads=gr, writes=xsr + sr)
            S.op("act", lambda e, tb=tb: e.activation(out=st[:, 68 + tb:69 + tb], in_=st[:, 64 + tb:65 + tb], func=AF.Sqrt, scale=1.0 / D, bias=RMS_EPS), reads=sr, writes=sr)
            S.op("dve", lambda e, tb=tb: e.reciprocal(out=st[:, 72 + tb:73 + tb], in_=st[:, 68 + tb:69 + tb]), reads=sr, writes=sr)
            if tb % 2 == 0:
                S.op("act", lambda e, xsa=xsa, tb=tb, ga=ga: e.activation(out=xsa, in_=ga, func=AF.Copy, scale=st[:, 72 + tb:73 + tb]),
                     reads=gr + sr, writes=xsr)
            else:
                S.op("dve", lambda e, xsa=xsa, tb=tb, ga=ga: e.tensor_scalar(out=xsa, in0=ga, scalar1=st[:, 72 + tb:73 + tb], scalar2=None, op0=ALU.mult),
                     reads=gr + sr, writes=xsr)

    def fe_b(self, xs, gbuf):
        S = self.S
        ia, ir = self.c_ident.all()
        ga, gr = gbuf.all()
        for pr in range(8):
            b = self.nextbank()
            pb = self.banks[b][:].bitcast(BF16)
            for c in range(2):
                cc = pr * 2 + c
                for tb in range(4):
                    xsa, xsr = xs.sub(tb * D + cc * 128, 128)
                    S.op("pe", lambda e, pb=pb, c=c, tb=tb, xsa=xsa: e.transpose(out=pb[:, c * 512 + tb * 128:c * 512 + (tb + 1) * 128], in_=xsa, identity=ia),
                         reads=xsr + ir, writes=[self.bres[b]])
            for c in range(2):
                cc = pr * 2 + c
                ha, hr = self.hT.sub(cc * 512, 512)
                if pr % 2 == 0:
                    S.op("dve", lambda e, pb=pb, c=c, cc=cc, ha=ha: e.tensor_scalar(out=ha, in0=pb[:, c * 512:(c + 1) * 512], scalar1=ga[:, cc:cc + 1], scalar2=None, op0=ALU.mult),
                         reads=[self.bres[b]] + gr, writes=hr)
                else:
                    S.op("act", lambda e, pb=pb, c=c, cc=cc, ha=ha: e.activation(out=ha, in_=pb[:, c * 512:(c + 1) * 512], func=AF.Copy, scale=ga[:, cc:cc + 1]),
                         reads=[self.bres[b]] + gr, writes=hr)

    def front_end(self, xsrc_rows, xs, gbuf, load):
        self.fe_a(xsrc_rows, xs, load)
        self.fe_b(xs, gbuf)

    def mm_fm(self, b, pbuf, j, act, KC, base=0):
        S = self.S
        ba = self.banks[b][:]
        for kc in range(KC):
            pa, pr = pbuf.sub(base + kc * 512 + j * 128, 128)
            aa, ar = act.sub(kc * 512, 512)
            S.op("pe", lambda e, pa=pa, aa=aa, kc=kc: e.matmul(ba, lhsT=pa, rhs=aa, start=(kc == 0), stop=(kc == KC - 1)),
                 reads=pr + ar, writes=[self.bres[b]])

    def mm_tm(self, b, act, tb, pbuf, KC):
        S = self.S
        ba = self.banks[b][:]
        for kc in range(KC):
            pa, pr = pbuf.sub(kc * 512, 512)
            aa, ar = act.sub(kc * 512 + tb * 128, 128)
            S.op("pe", lambda e, pa=pa, aa=aa, kc=kc: e.matmul(ba, lhsT=aa, rhs=pa, start=(kc == 0), stop=(kc == KC - 1)),
                 reads=pr + ar, writes=[self.bres[b]])

    def load_rope(self, rope_src, t0):
        S = self.S
        ra, rr = self.rope.all(parts=32)
        S.dma("sp", [(ra.rearrange("p (a t) -> p a t", a=2), rope_src[:, :, t0:t0 + T])], writes=rr, key="rope")

    def rotary_proj(self, grp, dst, evac_i, hook=None):
        S = self.S
        ra, rr = self.rope.all(parts=32)
        pw, pwr = self.c_pswap.all()

        def do_rot(hc, b, da, dr):
            ba = self.banks[b][:]
            b2 = self.nextbank()
            b2a = self.banks[b2][:]
            S.op("pe", lambda e, b2a=b2a, da=da: e.matmul(b2a, lhsT=pw, rhs=da, start=True, stop=True),
                 reads=pwr + dr, writes=[self.bres[b2]])
            t1a, t1r = self.rt[0].sub(0, 512, parts=32)
            t2a, t2r = self.rt[1].sub(0, 512, parts=32)
            S.op("dve", lambda e, t1a=t1a, b2a=b2a: e.tensor_tensor(out=t1a, in0=b2a[0:32, :], in1=ra[:, 512:1024], op=ALU.mult),
                 reads=[self.bres[b2]] + rr, writes=t1r)
            S.op("dve", lambda e, t2a=t2a, ba=ba: e.tensor_tensor(out=t2a, in0=ba[0:32, :], in1=ra[:, 0:512], op=ALU.mult),
                 reads=[self.bres[b]] + rr, writes=t2r)
            d32, d32r = dst.sub(hc * 512, 512, parts=32)
            S.op("pool", lambda e, d32=d32, t1a=t1a, t2a=t2a: e.tensor_tensor(out=d32, in0=t1a, in1=t2a, op=ALU.add),
                 reads=t1r + t2r, writes=d32r)

        pending = None
        for hc in range(8):
            if hc % 4 == 0:
                pbuf = self.panel("in", grp * 2 + hc // 4)
            b = self.nextbank()
            self.mm_fm(b, pbuf, hc % 4, self.hT, 16)
            ba = self.banks[b][:]
            da, dr = dst.sub(hc * 512, 512)
            if (hc + evac_i) % 2 == 0:
                S.op("act", lambda e, da=da, ba=ba: e.copy(out=da, in_=ba), reads=[self.bres[b]], writes=dr)
            else:
                S.op("dve", lambda e, da=da, ba=ba: e.tensor_copy(out=da, in_=ba), reads=[self.bres[b]], writes=dr)
            if pending is not None:
                do_rot(*pending)
            pending = (hc, b, da, dr)
            if hook is not None:
                hook(hc)
        do_rot(*pending)

    def kv_tile(self, sq, xsrc, rope_src, t, mid=None):
        S = self.S
        t0 = t * T
        self.load_rope(rope_src, t0)
        self.rotary_proj(1, self.kst, 0)
        pairs = []
        ka, kr = self.kst.all()
        for h in range(4):
            pairs.append((self.kscr[sq][h, :, :, t0:t0 + T], ka[:, h * 1024:(h + 1) * 1024].rearrange("p (c t) -> p c t", c=2)))
        S.dma("pool", pairs, reads=kr, writes=[self.kvres[sq]], key="kst")
        self.dump("kst", self.kst)
        if mid is not None:
            mid()
        va, vr = self.vst.all()
        v4 = va.rearrange("p (h t e) -> p h t e", h=4, t=4)
        i = 0
        for cg in range(2):
            pbuf = self.panel("in", 4 + cg)
            for tb in range(4):
                b = self.nextbank()
                self.mm_tm(b, self.hT, tb, pbuf, 16)
                ba = self.banks[b][:].rearrange("p (h e) -> p h e", h=2)
                dst = v4[:, cg * 2:cg * 2 + 2, tb, 0:256]
                if i % 2 == 0:
                    S.op("act", lambda e, dst=dst, ba=ba: e.copy(out=dst, in_=ba), reads=[self.bres[b]], writes=vr)
                else:
                    S.op("dve", lambda e, dst=dst, ba=ba: e.tensor_copy(out=dst, in_=ba), reads=[self.bres[b]], writes=vr)
                i += 1
        pairs = []
        for h in range(4):
            pairs.append((self.vscr[sq][h, :, t * 4:t * 4 + 4, :].rearrange("p t e -> p (t e)"), va[:, h * 1032:(h + 1) * 1032]))
        S.dma("pool", pairs, reads=vr, writes=[self.kvres[sq]], key="vst")
        self.dump("vst", self.vst)

    def main_tile_a(self, rope_src, t):
        self.load_rope(rope_src, t * T)
        self.fe_b(self.xsP, self.c_g1)

    def main_tile(self, sq, xsrc, rope_src, ydst, t, L, nxt):
        S = self.S
        t0 = t * T
        st, sr = self.c_stat.all()
        for tb in range(4):
            xta, xtr = self.x.sub(tb * D, D)
            S.dma("pool", [(xta, xsrc[t0 + tb * 128:t0 + (tb + 1) * 128, :])], writes=xtr, key=("xr", tb))
        for cg in range(2):
            pbuf = self.panel("in", 8 + cg)
            for tb in range(4):
                b = self.nextbank()
                self.mm_tm(b, self.hT, tb, pbuf, 16)
                ba = self.banks[b][:]
                da, dr = self.sv32.sub(tb * 1024 + cg * 512, 512)
                S.op("act", lambda e, da=da, ba=ba: e.activation(out=da, in_=ba, func=AF.Gelu), reads=[self.bres[b]], writes=dr)
        S.dma("sp", [(self.lng.all()[0], self.I["lng"]), (self.lnb.all()[0], self.I["lnb"])],
              writes=self.lng.all()[1] + self.lnb.all()[1], key="ln")
        lga, lgr = self.lng.all()
        lba, lbr = self.lnb.all()
        def ln_step(tb):
            sva, svr = self.sv32.sub(tb * 1024, 1024)
            S.op("dve", lambda e, sva=sva: e.bn_stats(out=st[:, 24:30], in_=sva[:, 0:512]), reads=svr, writes=sr)
            S.op("dve", lambda e, sva=sva: e.bn_stats(out=st[:, 30:36], in_=sva[:, 512:1024]), reads=svr, writes=sr)
            S.op("dve", lambda e: e.bn_aggr(out=st[:, 36:38], in_=st[:, 24:36]), reads=sr, writes=sr)
            S.op("act", lambda e: e.activation(out=st[:, 38:39], in_=st[:, 37:38], func=AF.Sqrt, scale=1.0, bias=LN_EPS), reads=sr, writes=sr)
            S.op("dve", lambda e: e.reciprocal(out=st[:, 39:40], in_=st[:, 38:39]), reads=sr, writes=sr)
            S.op("dve", lambda e, sva=sva: e.tensor_scalar(out=sva, in0=sva, scalar1=st[:, 36:37], scalar2=st[:, 39:40], op0=ALU.subtract, op1=ALU.mult),
                 reads=svr + sr, writes=svr)
            S.op("pool", lambda e, sva=sva: e.tensor_tensor(out=sva, in0=sva, in1=lga, op=ALU.mult), reads=svr + lgr, writes=svr)
            sna, snr = self.svn.sub(tb * 1024, 1024)
            S.op("pool", lambda e, sva=sva, sna=sna: e.tensor_tensor(out=sna, in0=sva, in1=lba, op=ALU.add), reads=svr + lbr, writes=snr)
        self.rotary_proj(0, self.qT, 1, hook=lambda hc: ln_step(hc // 2) if hc % 2 == 1 else None)
        self.dump("hT", self.hT)
        self.dump("qT", self.qT)
        for c in range(8):
            if c % 4 == 0:
                pbuf = self.panel("in", 6 + c // 4)
            b = self.nextbank()
            self.mm_fm(b, pbuf, c % 4, self.hT, 16)
            ba = self.banks[b][:]
            da, dr = self.suT.sub(c * 512, 512)
            S.op("act", lambda e, da=da, ba=ba: e.activation(out=da, in_=ba, func=AF.Gelu), reads=[self.bres[b]], writes=dr)
        wa, wr = self.c_sguw.all()
        sba, sbr = self.c_sgub.all()
        for tb in range(4):
            for half in range(2):
                b = self.nextbank()
                ba = self.banks[b][:]
                for gi in range(4):
                    g = half * 4 + gi
                    sna, snr = self.svn.sub(tb * 1024 + g * 128, 128)
                    S.op("pe", lambda e, ba=ba, gi=gi, g=g, sna=sna: e.matmul(ba[:, gi * 128:(gi + 1) * 128], lhsT=sna, rhs=wa[:, g * 128:(g + 1) * 128], start=True, stop=True),
                         reads=snr + wr, writes=[self.bres[b]])
                t1a, t1r = self.rt[half].all()
                S.op("dve", lambda e, t1a=t1a, ba=ba, half=half: e.tensor_tensor(out=t1a, in0=ba, in1=sba[:, half * 512:(half + 1) * 512], op=ALU.add),
                     reads=[self.bres[b]] + sbr, writes=t1r)
                oa, orr = self.osguT.all()
                sua, sur = self.suT.all()
                o3 = oa.rearrange("p (g t) -> p g t", g=8)[:, half * 4:half * 4 + 4, tb * 128:(tb + 1) * 128]
                s3 = sua.rearrange("p (g t) -> p g t", g=8)[:, half * 4:half * 4 + 4, tb * 128:(tb + 1) * 128]
                t3 = t1a.rearrange("p (g t) -> p g t", g=4)
                S.op("pool", lambda e, o3=o3, s3=s3, t3=t3: e.tensor_tensor(out=o3, in0=t3, in1=s3, op=ALU.mult),
                     reads=t1r + sur, writes=orr)
        self.dump("suT", self.suT)
        self.dump("svn", self.svn)
        self.dump("osguT", self.osguT)
        pend = self.attention_tile(sq, L)
        for c in range(16):
            if c % 4 == 0:
                p_ga = self.panel("in", 10 + c // 4)
                p_gs = self.panel("in", 14 + c // 4)
                p_ab = self.panel2("bra", "brs", c // 4)
            bG = self.nextbank()
            self.mm_fm(bG, p_ga, c % 4, self.hT, 16)
            bS = self.nextbank()
            self.mm_fm(bS, p_gs, c % 4, self.hT, 16)
            if c == 0:
                pend()
                self.dump("oattnT", self.oattnT)
            bB = self.nextbank()
            self.mm_fm(bB, p_ab, c % 4, self.osguT, 8, base=4096)
            bA = self.nextbank()
            self.mm_fm(bA, p_ab, c % 4, self.oattnT, 8)
            sa, sar = self.mtmp[0].all()
            ss, ssr = self.mtmp[1].all()
            m1, m1r = self.mtmp[2].all()
            m2, m2r = self.mtmp[3].all()
            S.op("act", lambda e, sa=sa, bG=bG: e.activation(out=sa, in_=self.banks[bG][:], func=AF.Sigmoid), reads=[self.bres[bG]], writes=sar)
            S.op("act", lambda e, ss=ss, bS=bS: e.activation(out=ss, in_=self.banks[bS][:], func=AF.Sigmoid), reads=[self.bres[bS]], writes=ssr)
            S.op("dve", lambda e, m1=m1, sa=sa, bA=bA: e.tensor_tensor(out=m1, in0=self.banks[bA][:], in1=sa, op=ALU.mult), reads=[self.bres[bA]] + sar, writes=m1r)
            S.op("dve", lambda e, m2=m2, ss=ss, bB=bB: e.tensor_tensor(out=m2, in0=self.banks[bB][:], in1=ss, op=ALU.mult), reads=[self.bres[bB]] + ssr, writes=m2r)
            ma, mr = self.mergedT.sub(c * 512, 512)
            S.op("pool", lambda e, ma=ma, m1=m1, m2=m2: e.tensor_tensor(out=ma, in0=m1, in1=m2, op=ALU.add), reads=m1r + m2r, writes=mr)
        for cg in range(4):
            pbuf = self.panel("out", cg)
            for tb in range(4):
                b = self.nextbank()
                self.mm_tm(b, self.mergedT, tb, pbuf, 16)
                xa, xr = self.x.sub(tb * D + cg * 512, 512)
                S.op("dve", lambda e, xa=xa, b=b: e.tensor_tensor(out=xa, in0=self.banks[b][:], in1=xa, op=ALU.add), reads=[self.bres[b]] + xr, writes=xr)
        self.dump("mergedT", self.mergedT)
        self.dump("x1", self.x)
        self.front_end(None, self.xs2, self.c_g2, False)
        self.dump("h2T", self.h2T)
        seq = ["u0", "u1", "d0", "u2", "d1", "u3", "d2", "d3"]
        ei = 0
        for item in seq:
            if item == "d0" and nxt is not None:
                self.fe_a_stream(nxt, (0, 1))
            if item == "d1" and nxt is not None:
                self.fe_a_stream(nxt, (2, 3))
            s = int(item[1])
            u = self.uT[s % 2]
            if item[0] == "u":
                for g in range(4):
                    pbuf = self.panel("up", s * 4 + g)
                    for j in range(4):
                        b = self.nextbank()
                        self.mm_fm(b, pbuf, j, self.h2T, 16)
                        ra, rr = self.relu[ei % 2].all()
                        ei += 1
                        S.op("act", lambda e, ra=ra, b=b: e.activation(out=ra, in_=self.banks[b][:], func=AF.Relu), reads=[self.bres[b]], writes=rr)
                        ua, ur = u.sub((g * 4 + j) * 512, 512)
                        S.op("pool", lambda e, ua=ua, ra=ra: e.tensor_tensor(out=ua, in0=ra, in1=ra, op=ALU.mult), reads=rr, writes=ur)
            else:
                for cg in range(4):
                    pbuf = self.panel("down", cg * 4 + s)
                    for tb in range(4):
                        b = self.nextbank()
                        self.mm_tm(b, u, tb, pbuf, 16)
                        xa, xr = self.x.sub(tb * D + cg * 512, 512)
                        S.op("dve", lambda e, xa=xa, b=b: e.tensor_tensor(out=xa, in0=self.banks[b][:], in1=xa, op=ALU.add), reads=[self.bres[b]] + xr, writes=xr)
        self.dump("x2", self.x)

    def final_norm(self, ydst, t):
        S = self.S
        t0 = t * T
        st, sr = self.c_stat.all()
        ga, gr = self.gfb.all()
        S.dma("pool", [(ga, self.I["gf"])], writes=gr, key="gf")
        for tb in range(4):
            xta, xtr = self.x.sub(tb * D, D)
            ja, jr = self.ost[tb % 2].all()
            S.op("act", lambda e, xta=xta, tb=tb, ja=ja: e.activation(out=ja, in_=xta, func=AF.Square, accum_out=st[:, 8 + tb:9 + tb]),
                 reads=xtr, writes=jr + sr)
        S.op("act", lambda e: e.activation(out=st[:, 12:16], in_=st[:, 8:12], func=AF.Sqrt, scale=1.0 / D, bias=RMS_EPS), reads=sr, writes=sr)
        S.op("dve", lambda e: e.reciprocal(out=st[:, 16:20], in_=st[:, 12:16]), reads=sr, writes=sr)
        for tb in range(4):
            xta, xtr = self.x.sub(tb * D, D)
            oa, orr = self.ost[tb % 2].all()
            S.op("dve", lambda e, xta=xta, oa=oa, tb=tb: e.scalar_tensor_tensor(out=oa, in0=xta, scalar=st[:, 16 + tb:17 + tb], in1=ga, op0=ALU.mult, op1=ALU.mult),
                 reads=xtr + sr + gr, writes=orr)
            self.stores.append(S.dma("pool", [(ydst[t0 + tb * 128:t0 + (tb + 1) * 128, :], oa)], reads=orr, key=("ost", tb % 2)))

    def attention_tile(self, sq, L):
        S = self.S
        st, sr = self.c_stat.all()
        nch = L // 128
        cpb = min(8, nch)
        kbs = cpb * 128
        ia, ir = self.c_ident.all()
        ona, onr = self.on.all()
        sga, sgr = self.c_subg.all()
        seq = [(h, c, i) for h in range(4) for c in range(2) for i in range(nch)]
        n = len(seq)
        kviews = {}
        vviews = {}

        def getk(h, c, blk):
            if (h, c, blk) not in kviews:
                kb = self.kblk[self.kptr % 4]
                self.kptr += 1
                a, r = kb.all()
                S.dma("sp", [(a[:, 0:kbs], self.kscr[sq][h, :, c, blk * kbs:(blk + 1) * kbs])], reads=[self.kvres[sq]], writes=r, key=("kb", (self.kptr - 1) % 4))
                kviews[(h, c, blk)] = kb
            return kviews[(h, c, blk)]

        def getv(h, c, blk):
            if (h, c, blk) not in vviews:
                vb = self.vblk[self.vptr % 3]
                self.vptr += 1
                a, r = vb.all()
                S.dma("sp", [(a[:, 0:cpb * 258], self.vscr[sq][h, :, blk * cpb:(blk + 1) * cpb, :].rearrange("p t e -> p (t e)"))], reads=[self.kvres[sq]], writes=r, key=("vb", (self.vptr - 1) % 3))
                vviews[(h, c, blk)] = vb
            return vviews[(h, c, blk)]

        def qk(g):
            h, c, i = seq[g]
            qa, qr = self.qT.sub((h * 2 + c) * 512, 512)
            kb = getk(h, c, i // cpb)
            ka, kr = kb.sub((i % cpb) * 128, 128)
            b = 4 + g % 3
            S.op("pe", lambda e, b=b, ka=ka, qa=qa: e.matmul(self.banks[b][:], lhsT=ka, rhs=qa, start=True, stop=True),
                 reads=kr + qr, writes=[self.bres[b]])

        def evac(h, c):
            o0a, o0r = self.o0[h % 2].all()
            for qb in range(4):
                acc = self.banks[qb][:]
                if c == 0:
                    S.op("dve", lambda e, acc=acc, qb=qb: e.reciprocal(out=st[:, 40 + qb:41 + qb], in_=acc[:, 256:257]), reads=[self.bres[qb]], writes=sr)
                    S.op("dve", lambda e, acc=acc, qb=qb, o0a=o0a: e.tensor_scalar(out=o0a[:, qb * 256:(qb + 1) * 256], in0=acc[:, 0:256], scalar1=st[:, 40 + qb:41 + qb], scalar2=None, op0=ALU.mult),
                         reads=[self.bres[qb]] + sr, writes=o0r)
                else:
                    S.op("dve", lambda e, acc=acc, qb=qb: e.reciprocal(out=st[:, 44 + qb:45 + qb], in_=acc[:, 256:257]), reads=[self.bres[qb]], writes=sr)
                    S.op("dve", lambda e, qb=qb: e.tensor_tensor(out=st[:, 48 + qb:49 + qb], in0=st[:, 44 + qb:45 + qb], in1=st[:, 5:6], op=ALU.mult), reads=sr, writes=sr)
                    S.op("dve", lambda e, acc=acc, qb=qb, o0a=o0a: e.scalar_tensor_tensor(out=o0a[:, qb * 256:(qb + 1) * 256], in0=acc[:, 0:256], scalar=st[:, 48 + qb:49 + qb], in1=o0a[:, qb * 256:(qb + 1) * 256], op0=ALU.mult, op1=ALU.add),
                         reads=[self.bres[qb]] + sr + o0r, writes=o0r)

        def fin_dve(h):
            o0a, o0r = self.o0[h % 2].all()
            for qb in range(4):
                S.op("act", lambda e, qb=qb, o0a=o0a: e.activation(out=ona[:, qb * 256:(qb + 1) * 256], in_=o0a[:, qb * 256:(qb + 1) * 256], func=AF.Square, accum_out=st[:, 52 + qb:53 + qb]),
                     reads=o0r, writes=onr + sr)
            S.op("act", lambda e: e.activation(out=st[:, 56:60], in_=st[:, 52:56], func=AF.Sqrt, scale=1.0 / 256, bias=RMS_EPS), reads=sr, writes=sr)
            S.op("dve", lambda e: e.reciprocal(out=st[:, 60:64], in_=st[:, 56:60]), reads=sr, writes=sr)
            for qb in range(4):
                S.op("dve", lambda e, qb=qb, o0a=o0a: e.tensor_scalar(out=ona[:, qb * 256:(qb + 1) * 256], in0=o0a[:, qb * 256:(qb + 1) * 256], scalar1=st[:, 60 + qb:61 + qb], scalar2=None, op0=ALU.mult),
                     reads=o0r + sr, writes=onr)

        def fin_pe(h):
            b = 7
            pb = self.banks[b][:].bitcast(BF16)
            for qb in range(4):
                for j in range(2):
                    S.op("pe", lambda e, qb=qb, j=j: e.transpose(out=pb[:, j * 512 + qb * 128:j * 512 + (qb + 1) * 128], in_=ona[:, qb * 256 + j * 128:qb * 256 + (j + 1) * 128], identity=ia),
                         reads=onr + ir, writes=[self.bres[b]])
            for j in range(2):
                oa, orr = self.oattnT.sub((h * 2 + j) * 512, 512)
                S.op("dve", lambda e, oa=oa, j=j: e.tensor_scalar(out=oa, in0=pb[:, j * 512:(j + 1) * 512], scalar1=sga[:, j:j + 1], scalar2=1.0 - LAMBDA_INIT, op0=ALU.mult, op1=ALU.mult),
                     reads=[self.bres[b]] + sgr, writes=orr)

        LA = 2
        for g in range(min(LA, n)):
            qk(g)
        pending_evac = None
        deferred = {}
        for g in range(n):
            h, c, i = seq[g]
            if g + LA < n:
                qk(g + LA)
            b = 4 + g % 3
            ea, er = self.E[g % 4].all()
            S.op("act", lambda e, b=b, ea=ea: e.activation(out=ea, in_=self.banks[b][:], func=AF.Exp, scale=ATTN_SCALE),
                 reads=[self.bres[b]], writes=er)
            if i == 0 and pending_evac is not None:
                evac(*pending_evac)
                pending_evac = None
            for fn in deferred.pop(g, []):
                fn()
            vb = getv(h, c, i // cpb)
            va, vr = vb.sub((i % cpb) * 258, 257)
            for qb in range(4):
                S.op("pe", lambda e, qb=qb, ea=ea, va=va, i=i: e.matmul(self.banks[qb][:, 0:257], lhsT=ea[:, qb * 128:(qb + 1) * 128], rhs=va, start=(i == 0), stop=(i == nch - 1)),
                     reads=er + vr, writes=[self.bres[qb]])
            if i == nch - 1:
                pending_evac = (h, c)
                if c == 1 and h < 3:
                    deferred.setdefault(g + 3, []).append(lambda h=h: fin_dve(h))
                    deferred.setdefault(g + 8, []).append(lambda h=h: fin_pe(h))
        evac(*pending_evac)
        for g in sorted(deferred):
            for fn in deferred[g]:
                fn()
        fin_dve(3)
        return lambda: fin_pe(3)

    kptr = 0
    vptr = 0


def _rope_tables(positions):
    rot = 32
    inv = (1.0 / (np.float32(ROPE_THETA) ** (np.arange(0, rot, 2, dtype=np.float32) / np.float32(rot)))).astype(np.float32)
    ang = positions.astype(np.float32)[None, :] * inv[:, None]
    c = np.cos(ang).astype(np.float32)
    s = np.sin(ang).astype(np.float32)
    C = np.concatenate([c, c], axis=0)
    Sg = np.concatenate([-s, s], axis=0)
    return np.ascontiguousarray(np.stack([C, Sg], axis=1))


_NC_CACHE = {}


def run(inp, NP, SS, n_prompt=8, n_sample=2):
    NS = SS // 4
    key = (NP, SS)
    if key not in _NC_CACHE:
        _NC_CACHE[key] = Builder(NP, SS).build()
    nc = _NC_CACHE[key]
    f = lambda a: np.ascontiguousarray(np.asarray(a, dtype=np.float32))
    bc = lambda v: np.ascontiguousarray(np.broadcast_to(f(v).reshape(1, -1), (128, f(v).size)))
    common = {
        "w_in": f(inp["w_in"][0]), "w_bra": f(inp["w_br_attn"][0]), "w_brs": f(inp["w_br_sgu"][0]),
        "w_out": f(inp["w_out"][0]), "w_up": f(inp["w_up"][0]), "w_down": f(inp["w_down"][0]),
        "g1": np.ascontiguousarray(f(inp["attn_norm_g"][0]).reshape(16, 128).T),
        "g2": np.ascontiguousarray(f(inp["mlp_norm_g"][0]).reshape(16, 128).T),
        "gf": bc(inp["final_norm_g"]),
        "lamv": np.ascontiguousarray(np.concatenate([bc(inp["lambda_q1"][0]), bc(inp["lambda_k1"][0]), bc(inp["lambda_q2"][0]), bc(inp["lambda_k2"][0])], axis=1)),
        "subg": np.ascontiguousarray(f(inp["subln_g"][0]).reshape(2, 128).T),
        "lng": bc(inp["sgu_ln_g"][0]), "lnb": bc(inp["sgu_ln_b"][0]),
        "sguw": np.ascontiguousarray(np.transpose(f(inp["sgu_w"][0]), (2, 0, 1)).reshape(128, 1024)),
        "sgub": bc(f(inp["sgu_b"][0]).reshape(-1)),
        "ident": np.eye(128, dtype=np.float32).astype(ml_dtypes.bfloat16),
        "ropeP": _rope_tables(np.arange(NP)),
        "ropeSF": _rope_tables(np.arange(SS)),
    }
    psw = np.zeros((128, 128), np.float32)
    for m in range(32):
        psw[(m + 16) if m < 16 else (m - 16), m] = 1.0
    common["pswap"] = psw.astype(ml_dtypes.bfloat16)
    xp = f(inp["x_prompt"])
    xs = f(inp["x_sample"])
    in_maps = []
    for core in range(8):
        si, qi = core // 4, core % 4
        if n_sample == 1:
            si = 0
        m = dict(common)
        m["xp"] = xp[core % n_prompt]
        m["xsf"] = xs[si]
        m["xso"] = np.ascontiguousarray(xs[si, qi * NS:(qi + 1) * NS])
        m["ropeSO"] = _rope_tables(np.arange(qi * NS, (qi + 1) * NS))
        in_maps.append(m)
    res = run_bass_kernel_spmd(nc, in_maps, core_ids=list(range(8)))
    global LAST_RES
    LAST_RES = res
    yp = np.stack([np.asarray(res.results[c]["yp"], dtype=np.float32) for c in range(n_prompt)], axis=0)
    ys = np.stack([np.concatenate([np.asarray(res.results[si * 4 + qi]["ys"], dtype=np.float32) for qi in range(4)], axis=0)
                   for si in range(n_sample)], axis=0)
    return yp, ys


def kernel(**inputs):
    yp, ys = run(inputs, 2048, 8192)
    return (yp, ys)
```
